# Optimizing a Trainium2 kernel written in Bass

```python
import jax, jax.numpy as jnp
from jax import lax
import numpy as np

D_MODEL = 4096
BATCH = 4
SEQ = 2048
DEPTH = 1
DEC_BATCH = 2
DEC_SEQ = 4096
PAST_LEN = 128

HEAD_DIM = 128
HEADS_PER_GROUP = 8
DILATED_GROUPS = ((128, 1), (512, 4), (2048, 16))
N_ATT_GROUPS = 3
ATT_QKV_WIDTH = N_ATT_GROUPS * HEADS_PER_GROUP * HEAD_DIM
ATT_OUT_WIDTH = HEADS_PER_GROUP * HEAD_DIM
ROPE_DIM = HEAD_DIM // 4
ROPE_THETA = 500000.0
NEG_INF = -1e30
SGU_WIDTH = D_MODEL
SGU_CHUNK = 128
SGU_GROUPS = 16
SGU_GROUP_DIM = SGU_WIDTH // SGU_GROUPS
D_FF = -(-8 * D_MODEL // (3 * 256)) * 256
IN_WIDTH = 3 * ATT_QKV_WIDTH + 2 * SGU_WIDTH + 2 * D_MODEL
NORM_EPS = 1e-6

kernel_name = "hybrid_dilated_attn_gmlp_encoder"


def rmsnorm(x, g):
    xf = x.astype(jnp.float32)
    y = xf * lax.rsqrt(jnp.mean(xf * xf, axis=-1, keepdims=True) + NORM_EPS)
    return (y * g.astype(jnp.float32)).astype(x.dtype)


def partial_rotary(x, pos):
    half = ROPE_DIM // 2
    inv_freq = ROPE_THETA ** (-jnp.arange(0, ROPE_DIM, 2, dtype=jnp.float32) / ROPE_DIM)
    ang = pos[:, None] * inv_freq[None, :]
    cos = jnp.cos(ang)[None, :, None, :]
    sin = jnp.sin(ang)[None, :, None, :]
    xf = x[..., :ROPE_DIM].astype(jnp.float32)
    x1, x2 = xf[..., :half], xf[..., half:]
    rot = jnp.concatenate([x1 * cos - x2 * sin, x2 * cos + x1 * sin], axis=-1)
    return jnp.concatenate([rot.astype(x.dtype), x[..., ROPE_DIM:]], axis=-1)


def dilated_window_attention(q, k, v, dil, radius):
    B, S, H, Dh = q.shape
    L = S // dil
    blk = radius
    nb = -(-L // blk)
    Lp = nb * blk
    def sub(t):
        return t.reshape(B, L, dil, H, Dh)
    qs = jnp.pad(sub(q), ((0, 0), (0, Lp - L), (0, 0), (0, 0), (0, 0))).reshape(B, nb, blk, dil, H, Dh)
    def windows(t):
        tp = jnp.pad(sub(t), ((0, 0), (blk, Lp - L + blk), (0, 0), (0, 0), (0, 0)))
        tb = tp.reshape(B, nb + 2, blk, dil, H, Dh)
        return jnp.concatenate([tb[:, :-2], tb[:, 1:-1], tb[:, 2:]], axis=2)
    kw = windows(k)
    vw = windows(v)
    scale = 1.0 / np.sqrt(Dh)
    s = jnp.einsum("bnqrhd,bnkrhd->bnrhqk", qs, kw, preferred_element_type=jnp.float32) * scale
    qpos = jnp.arange(nb)[:, None] * blk + jnp.arange(blk)[None, :]
    kpos = jnp.arange(nb)[:, None] * blk - blk + jnp.arange(3 * blk)[None, :]
    valid = ((jnp.abs(qpos[:, :, None] - kpos[:, None, :]) <= radius)
             & (kpos >= 0)[:, None, :] & (kpos < L)[:, None, :])
    s = jnp.where(valid[None, :, None, None], s, NEG_INF)
    m = jnp.max(s, axis=-1)
    p = jnp.exp(s - m[..., None])
    l = jnp.sum(p, axis=-1)
    o = jnp.einsum("bnrhqk,bnkrhd->bnqrhd", p, vw.astype(jnp.float32))
    o = o.reshape(B, Lp, dil, H, Dh)[:, :L].reshape(B, S, H, Dh)
    def back(t):
        return jnp.transpose(t, (0, 1, 4, 2, 3)).reshape(B, Lp, dil, H)[:, :L].reshape(B, S, H)
    return o, back(m), back(l)


def encoder_layer(x, attn_norm, w_in, sgu_norm, sgu_w, sgu_b, w_branch, w_out,
                  ffn_norm, w_gate, w_up, w_down):
    B, S, _ = x.shape
    h = rmsnorm(x, attn_norm)
    z = h @ w_in
    a = ATT_QKV_WIDTH
    splits = [a, 2 * a, 3 * a, 3 * a + SGU_WIDTH, 3 * a + 2 * SGU_WIDTH,
              3 * a + 2 * SGU_WIDTH + D_MODEL]
    q, k, v, u, vs, g_a, g_b = jnp.split(z, splits, axis=-1)
    q = q.reshape(B, S, N_ATT_GROUPS, HEADS_PER_GROUP, HEAD_DIM)
    k = k.reshape(B, S, N_ATT_GROUPS, HEADS_PER_GROUP, HEAD_DIM)
    v = v.reshape(B, S, N_ATT_GROUPS, HEADS_PER_GROUP, HEAD_DIM)
    pos = jnp.arange(S, dtype=jnp.float32)

    outs, maxes, dens = [], [], []
    for g, (window, dil) in enumerate(DILATED_GROUPS):
        radius = window // (2 * dil)
        o_g, m_g, l_g = dilated_window_attention(partial_rotary(q[:, :, g], pos),
                                                 partial_rotary(k[:, :, g], pos),
                                                 v[:, :, g], dil, radius)
        outs.append(o_g); maxes.append(m_g); dens.append(l_g)
    m_all = jnp.stack(maxes)
    wts = jnp.exp(m_all - jnp.max(m_all, axis=0, keepdims=True))
    num = jnp.sum(wts[..., None] * jnp.stack(outs), axis=0)
    den = jnp.sum(wts * jnp.stack(dens), axis=0)
    attn_out = (num / den[..., None]).reshape(B, S, ATT_OUT_WIDTH).astype(x.dtype)

    u = jax.nn.gelu(u, approximate=False)
    vs = rmsnorm(jax.nn.gelu(vs, approximate=False), sgu_norm)
    vc = vs.reshape(B, S // SGU_CHUNK, SGU_CHUNK, SGU_GROUPS, SGU_GROUP_DIM)
    sp = jnp.einsum("gts,bnsgc->bntgc", sgu_w, vc) + sgu_b.T[:, :, None]
    sgu_out = u * sp.reshape(B, S, SGU_WIDTH)

    branch_a = attn_out @ w_branch[:ATT_OUT_WIDTH]
    branch_b = sgu_out @ w_branch[ATT_OUT_WIDTH:]
    merged = jax.nn.sigmoid(g_a) * branch_a + jax.nn.sigmoid(g_b) * branch_b
    x = x + merged @ w_out

    h = rmsnorm(x, ffn_norm)
    x = x + (jax.nn.silu(h @ w_gate) * (h @ w_up)) @ w_down
    return x


def trunk(x, attn_norm, w_in, sgu_norm, sgu_w, sgu_b, w_branch, w_out,
          ffn_norm, w_gate, w_up, w_down, final_norm):
    for i in range(DEPTH):
        x = encoder_layer(x, attn_norm[i], w_in[i], sgu_norm[i], sgu_w[i], sgu_b[i],
                          w_branch[i], w_out[i], ffn_norm[i], w_gate[i], w_up[i], w_down[i])
    return rmsnorm(x, final_norm)


def setup_inputs(seed: int = 0) -> dict:
    key = jax.random.key(seed)
    ks = jax.random.split(key, 16)
    f32 = jnp.float32
    def nrm(k, shape, scale):
        return jax.random.normal(k, shape, f32) * scale
    w_branch = jnp.concatenate([
        nrm(ks[6], (DEPTH, ATT_OUT_WIDTH, D_MODEL), ATT_OUT_WIDTH ** -0.5),
        nrm(ks[7], (DEPTH, SGU_WIDTH, D_MODEL), SGU_WIDTH ** -0.5)], axis=1)
    return {
        "x_prompt": jax.random.normal(ks[0], (BATCH, SEQ, D_MODEL), f32),
        "x_sample": jax.random.normal(ks[1], (DEC_BATCH, DEC_SEQ, D_MODEL), f32),
        "attn_norm": 1.0 + nrm(ks[2], (DEPTH, D_MODEL), 0.02),
        "w_in": nrm(ks[3], (DEPTH, D_MODEL, IN_WIDTH), D_MODEL ** -0.5),
        "sgu_norm": 1.0 + nrm(ks[4], (DEPTH, SGU_WIDTH), 0.02),
        "sgu_w": nrm(ks[5], (DEPTH, SGU_GROUPS, SGU_CHUNK, SGU_CHUNK), SGU_CHUNK ** -0.5),
        "sgu_b": 1.0 + nrm(ks[8], (DEPTH, SGU_GROUPS, SGU_CHUNK), 0.02),
        "w_branch": w_branch,
        "w_out": nrm(ks[9], (DEPTH, D_MODEL, D_MODEL), D_MODEL ** -0.5),
        "ffn_norm": 1.0 + nrm(ks[10], (DEPTH, D_MODEL), 0.02),
        "w_gate": nrm(ks[11], (DEPTH, D_MODEL, D_FF), D_MODEL ** -0.5),
        "w_up": nrm(ks[12], (DEPTH, D_MODEL, D_FF), D_MODEL ** -0.5),
        "w_down": nrm(ks[13], (DEPTH, D_FF, D_MODEL), D_FF ** -0.5),
        "final_norm": 1.0 + nrm(ks[14], (D_MODEL,), 0.02),
    }


def reference(x_prompt, x_sample, attn_norm, w_in, sgu_norm, sgu_w, sgu_b, w_branch, w_out,
              ffn_norm, w_gate, w_up, w_down, final_norm):
    y_prompt = trunk(x_prompt, attn_norm, w_in, sgu_norm, sgu_w, sgu_b, w_branch, w_out,
                     ffn_norm, w_gate, w_up, w_down, final_norm)
    y_sample = trunk(x_sample, attn_norm, w_in, sgu_norm, sgu_w, sgu_b, w_branch, w_out,
                     ffn_norm, w_gate, w_up, w_down, final_norm)
    return (y_prompt, y_sample)
```

```python
import contextlib
import numpy as np
import concourse.bass as bass
import concourse.mybir as mybir
from concourse.bass_utils import run_bass_kernel_spmd

F32 = mybir.dt.float32
BF16 = mybir.dt.bfloat16
AF = mybir.ActivationFunctionType
ALU = mybir.AluOpType

D = 4096
NTOK = 2048
HALO = 1024
PADL = 1024
DFF = 11008
INW = 25600
EPS = 1e-6
SCALE = 1.0 / np.sqrt(128.0)
MASKV = -30000.0
DILS = (1, 4, 16)
SAME_SYNC = True
ARENA_BYTES = 207 * 1024

CG_Q, CG_K, CG_V, CG_U, CG_VS, CG_GA, CG_GB = range(7)


def cg_type(cg):
    if cg < 6:
        return CG_Q, cg
    if cg < 12:
        return CG_K, cg - 6
    if cg < 18:
        return CG_V, cg - 12
    if cg < 26:
        return CG_U, cg - 18
    if cg < 34:
        return CG_VS, cg - 26
    if cg < 42:
        return CG_GA, cg - 34
    return CG_GB, cg - 42


class Res:
    __slots__ = ("name", "w", "r", "multi", "excl")

    def __init__(self, name, multi=False, excl=False):
        self.name = name
        self.w = {}
        self.r = {}
        self.multi = multi
        self.excl = excl


class Prog:
    ENGS = ("pe", "act", "dve", "pool", "sp")

    def __init__(self, nc, es):
        self.nc = nc
        self.es = es
        self.q = {e: [] for e in self.ENGS}
        self.tick = {e: 0 for e in self.ENGS}
        self.waited = {e: {} for e in self.ENGS}
        self.esem = {e: es.enter_context(nc.semaphore("sem_" + e)) for e in self.ENGS}
        self.dsems = []

    def new_dsem(self, name):
        h = self.es.enter_context(self.nc.semaphore("d_" + name))
        self.dsems.append([h, 0])
        return len(self.dsems) - 1

    def _deps(self, eng, reads, writes):
        deps = {}
        for r in reads:
            for k, v in r.w.items():
                if deps.get(k, 0) < v:
                    deps[k] = v
        for w in writes:
            for k, v in w.r.items():
                if deps.get(k, 0) < v:
                    deps[k] = v
            if not w.multi:
                for k, v in w.w.items():
                    if deps.get(k, 0) < v:
                        deps[k] = v
        wt = self.waited[eng]
        for k, v in deps.items():
            if k == ("e", eng) and not SAME_SYNC:
                continue
            if wt.get(k, 0) < v:
                wt[k] = v
                self.q[eng].append(("w", k, v))

    @staticmethod
    def _mark(k, v, reads, writes):
        for w in writes:
            if w.multi:
                if w.w.get(k, 0) < v:
                    w.w[k] = v
            else:
                w.w = {k: v}
                w.r = {}
        for r in reads:
            if r.r.get(k, 0) < v:
                r.r[k] = v

    def op(self, eng, fn, reads=(), writes=()):
        ex = [r for r in reads if r.excl]
        if ex:
            reads = [r for r in reads if not r.excl]
            writes = list(writes) + [r for r in ex if r not in writes]
        self._deps(eng, reads, writes)
        self.tick[eng] += 1
        self.q[eng].append(("o", fn))
        self._mark(("e", eng), self.tick[eng], reads, writes)

    def dma(self, qeng, out, in_, reads, writes, sem):
        self._deps(qeng, reads, writes)
        self.dsems[sem][1] += 16
        self.q[qeng].append(("d", out, in_, sem))
        self._mark(("d", sem), self.dsems[sem][1], reads, writes)

    def barrier(self):
        for e in self.ENGS:
            wt = self.waited[e]
            for f in self.ENGS:
                k = ("e", f)
                if f != e and wt.get(k, 0) < self.tick[f]:
                    wt[k] = self.tick[f]
                    self.q[e].append(("w", k, self.tick[f]))
            for i, (_, c) in enumerate(self.dsems):
                k = ("d", i)
                if wt.get(k, 0) < c:
                    wt[k] = c
                    self.q[e].append(("w", k, c))

    def emit(self):
        nc = self.nc

        def body_for(name):
            def body(e):
                for it in self.q[name]:
                    if it[0] == "w":
                        k, v = it[1], it[2]
                        sem = self.esem[k[1]] if k[0] == "e" else self.dsems[k[1]][0]
                        e.wait_ge(sem, v)
                    elif it[0] == "o":
                        it[1](e).then_inc(self.esem[name], 1)
                    else:
                        e.dma_start(out=it[1], in_=it[2]).then_inc(self.dsems[it[3]][0], 16)
            return body

        with nc.Block() as block:
            block.tensor(body_for("pe"))
            block.scalar(body_for("act"))
            block.vector(body_for("dve"))
            block.gpsimd(body_for("pool"))
            block.sync(body_for("sp"))


class Arena:
    def __init__(self, t, nbytes):
        self.t = t
        self.n = nbytes
        self.off = 0

    def alloc(self, shape, dt):
        esz = 2 if dt == BF16 else 4
        n = int(np.prod(shape)) * esz
        n = (n + 63) // 64 * 64
        assert self.off + n <= self.n, ("arena overflow", self.off, n)
        a = self.t[:, self.off // 2:(self.off + n) // 2]
        if dt == F32:
            a = a.bitcast(F32)
        a = a[:, 0:int(np.prod(shape))]
        self.off += n
        if len(shape) == 2:
            a = a.rearrange("p (a b) -> p a b", a=shape[0])
        elif len(shape) == 3:
            a = a.rearrange("p (a b c) -> p a b c", a=shape[0], b=shape[1])
        return a


def build_program(debug=False, stop_after=99, p1_mode=None):
    nc = bass.Bass("TRN2", target_bir_lowering=False)
    kin = "ExternalInput"
    skind = "ExternalOutput" if debug else "Internal"

    def din(name, shape, dt=F32):
        return nc.dram_tensor(name, list(shape), dt, kind=kin).ap()

    x_main = din("x_main", [NTOK, D])
    x_halo = din("x_halo", [HALO, D])
    w_in = din("w_in", [50, 128, 32 * 512])
    if stop_after <= 2:
        w_br = w_out = w_gate = w_up = w_down = None
    else:
        w_br = din("w_branch", [16, 128, 40 * 256])
        w_out = din("w_out", [8, 128, 32 * 512])
        w_gate = din("w_gate", [43, 128, 32 * 256])
        w_up = din("w_up", [43, 128, 32 * 256])
        w_down = din("w_down", [16, 128, 43 * 512])
    n_attn = din("attn_norm", [128, 32])
    n_sgu = din("sgu_norm", [D])
    n_ffn = din("ffn_norm", [128, 32])
    n_fin = din("final_norm", [D])
    sgu_wT = din("sgu_wT", [128, 16 * 128])
    sgu_b = din("sgu_b", [1, 16 * 128])
    rope_in = din("rope", [128, 24 * 64])
    valid_in = din("validcol", [128, 69])
    consts_in = din("consts", [128, 4 * 128])
    y_out = nc.dram_tensor("y", [NTOK, D], F32, kind="ExternalOutput").ap()

    def dscr(name, shape, dt):
        return nc.dram_tensor(name, list(shape), dt, kind=skind).ap()

    qT_s = dscr("qT_s", [24, 128, NTOK], BF16)
    kT_s = dscr("kT_s", [24, 128, 4096], BF16)
    v_s = dscr("v_s", [4096, 3072], BF16)
    guT_s = dscr("guT_s", [D, NTOK], BF16)
    gvs_s = dscr("gvs_s", [NTOK, D], BF16)
    sgaT_s = dscr("sgaT_s", [D, NTOK], BF16)
    sgbT_s = dscr("sgbT_s", [D, NTOK], BF16)
    x1_s = dscr("x1_s", [NTOK, D], F32)
    actT_s = dscr("actT_s", [DFF, NTOK], BF16)
    p1_s = dscr("p1_s", [NTOK, D], F32)
    x2_s = dscr("x2_s", [NTOK, D], F32)
    w_br_bf = dscr("w_br_bf", [16, 128, 40 * 256], BF16)
    w_out_bf = dscr("w_out_bf", [8, 128, 32 * 512], BF16)
    dbg = {}
    if debug:
        dbg["attnT"] = dscr("attnT_d", [128, 8 * NTOK], BF16)

    with contextlib.ExitStack() as es:
        P = Prog(nc, es)
        arena_t = es.enter_context(nc.sbuf_tensor("arena", [128, ARENA_BYTES // 2], BF16))
        AR = Arena(arena_t, ARENA_BYTES)
        banks = [es.enter_context(nc.psum_tensor("ps%d" % i, [128, 512], F32)) for i in range(8)]
        bankR = [Res("bank%d" % i, excl=True) for i in range(8)]

        def bank_f32(i):
            return banks[i][:, :]

        def bank_bf(i):
            return banks[i][:, :].bitcast(BF16)

        R_in = Res("inputs", multi=True)
        R_q = Res("qT_s", multi=True)
        R_k = Res("kT_s", multi=True)
        R_v = Res("v_s", multi=True)
        R_gu = Res("guT_s", multi=True)
        R_gvs = Res("gvs_s", multi=True)
        R_sga = Res("sgaT_s", multi=True)
        R_sgb = Res("sgbT_s", multi=True)
        R_x1 = Res("x1_s", multi=True)
        R_act = Res("actT_s", multi=True)
        R_p1 = Res("p1_s", multi=True)
        R_x2 = Res("x2_s", multi=True)
        R_y = Res("y", multi=True)
        R_dbg = Res("dbg", multi=True)
        R_wbf = Res("wbf", multi=True)

        cst = AR.alloc([4, 128], BF16)
        ident = cst[:, 0, :]
        mask01 = cst[:, 1:3, :]
        ones_bf = cst[:, 3, :]
        rope = AR.alloc([24, 64], F32)
        validcol = AR.alloc([69], F32)
        ssq_vs = AR.alloc([16, 8], F32)
        small = AR.alloc([64], F32)
        gcol = AR.alloc([32], F32)
        R_gcol = Res("gcol")
        s_gcol = P.new_dsem("gcol")
        zt = AR.alloc([2048], BF16)
        PERSIST0 = AR.off
        attnT = AR.alloc([8, NTOK], BF16)
        R_cst = Res("cst")
        R_rope = Res("rope")
        R_valid = Res("validcol")
        R_ssqvs = Res("ssq_vs", multi=True)
        R_attnT = [Res("attnT%d" % j) for j in range(8)]
        PERSIST = AR.off

        s_misc = P.new_dsem("misc")
        P.dma("pool", cst.rearrange("p a b -> p (a b)"), consts_in, [R_in], [R_cst], s_misc)
        s_misc2 = P.new_dsem("misc2")
        P.dma("sp", rope.rearrange("p a b -> p (a b)"), rope_in, [R_in], [R_rope], s_misc2)
        s_misc3 = P.new_dsem("misc3")
        P.dma("sp", validcol, valid_in, [R_in], [R_valid], s_misc3)

        R_zt = Res("zt")
        P.op("dve", lambda e: e.memset(zt, 0.0), [], [R_zt])
        s_z = P.new_dsem("zero")
        for h0 in range(0, 24, 2):
            P.dma("sp", kT_s[h0:h0 + 2, :, 0:PADL].rearrange("h d c -> d h c"),
                  zt.rearrange("p (h c) -> p h c", h=2), [R_zt], [R_k], s_z)
        for r0 in range(0, PADL, 128):
            P.dma("sp", v_s[r0:r0 + 128, 0:2048], zt[:, 0:2048], [R_zt], [R_v], s_z)
            P.dma("sp", v_s[r0:r0 + 128, 2048:3072], zt[:, 0:1024], [R_zt], [R_v], s_z)
        if stop_after == 0:
            P.barrier()
            P.emit()
            return nc

        def prologue_tile(src_ap, R_src, xs, R_xs, s_xs, xn, R_xn, dstT, R_dst, tcol, stat_col, tb):
            P.dma("sp", xs, src_ap, [R_src], [R_xs], s_xs)
            ssq = small[:, stat_col:stat_col + 1]
            rstd = small[:, stat_col + 1:stat_col + 2]
            R_st = Res("st")
            P.op("act", lambda e: e.activation(out=xn, in_=xs, func=AF.Square, accum_out=ssq), [R_xs], [R_xn, R_st])
            P.op("dve", lambda e: e.tensor_scalar(out=rstd, in0=ssq, scalar1=1.0 / D, scalar2=EPS, op0=ALU.mult, op1=ALU.add), [R_st], [R_st])
            P.op("act", lambda e: e.activation(out=rstd, in_=rstd, func=AF.Sqrt), [R_st], [R_st])
            P.op("dve", lambda e: e.reciprocal(out=rstd, in_=rstd), [R_st], [R_st])
            P.op("act", lambda e: e.activation(out=xn, in_=xs, func=AF.Copy, scale=rstd), [R_xs, R_st], [R_xn])
            for q4 in range(4):
                bi = tb[q4 % 2]
                pv = bank_bf(bi).rearrange("p (a b) -> p a b", a=8)

                def tr(e, q4=q4, pv=pv):
                    ins = None
                    for i in range(8):
                        kc = q4 * 8 + i
                        ins = e.transpose(pv[:, i, :], xn[:, kc * 128:(kc + 1) * 128], ident)
                    return ins
                P.op("pe", tr, [R_xn, R_cst], [bankR[bi]])
                dst = dstT[:, q4 * 8:(q4 + 1) * 8, tcol:tcol + 128]
                gb = gcol[:, q4 * 8:(q4 + 1) * 8].unsqueeze(2).broadcast_to([128, 8, 128])
                P.op("dve", lambda e, dst=dst, pv=pv, gb=gb: e.tensor_tensor(out=dst, in0=pv, in1=gb, op=ALU.mult),
                     [bankR[bi], R_gcol], [R_dst])

        def load_bc(vec_ap, dst, R_dst, sem):
            P.dma("sp", dst, vec_ap.partition_broadcast(128), [R_in], [R_dst], sem)

        AR.off = PERSIST0
        hT = AR.alloc([32, 1024], BF16)
        wslot = [AR.alloc([32, 512], BF16) for _ in range(2)]
        R_ws = [Res("ws%d" % i) for i in range(2)]
        s_ws = [P.new_dsem("ws%d" % i) for i in range(2)]
        xs = [AR.alloc([D], F32) for _ in range(2)]
        R_xs = [Res("xs%d" % i) for i in range(2)]
        s_xs = [P.new_dsem("xs%d" % i) for i in range(2)]
        xn = [AR.alloc([D], BF16) for _ in range(2)]
        R_xn = [Res("xn%d" % i) for i in range(2)]
        R_hT = [Res("hT%d" % i) for i in range(8)]
        NST = 3
        stT = [AR.alloc([512], BF16) for _ in range(NST)]
        R_stT = [Res("stT%d" % i) for i in range(NST)]
        s_stT = [P.new_dsem("stT%d" % i) for i in range(NST)]
        stQ = [AR.alloc([4, 128], BF16) for _ in range(2)]
        R_stQ = [Res("stQ%d" % i) for i in range(2)]
        s_stQ = [P.new_dsem("stQ%d" % i) for i in range(2)]
        qk_tm = [AR.alloc([4, 128], BF16) for _ in range(2)]
        R_qk = [Res("qk%d" % i) for i in range(2)]
        ta = [AR.alloc([4, 32], F32) for _ in range(2)]
        tb_ = [AR.alloc([4, 32], F32) for _ in range(2)]
        R_ta = [Res("ta%d" % i) for i in range(2)]
        R_tb = [Res("tb%d" % i) for i in range(2)]
        tmpf = [AR.alloc([512], F32) for _ in range(2)]
        R_tmpf = [Res("tmpf%d" % i) for i in range(2)]
        junk = AR.alloc([512], BF16)
        R_junk = Res("junk", multi=True)
        print("phase1 arena", AR.off)

        P.dma("sp", gcol, n_attn, [R_in], [R_gcol], s_gcol)
        P.op("dve", lambda e: e.memset(ssq_vs.rearrange("p a b -> p (a b)"), 0.0), [], [R_ssqvs])

        cnt = {"ws": 0, "bank": 0, "st": 0, "stq": 0, "qk": 0, "tmp": 0, "x": 0}
        pend_qk = []

        def flush_qk():
            while pend_qk:
                pend_qk.pop(0)()
        KV_CGS = list(range(6, 18))
        blocks = [
            ("halo", NTOK, 8, [(cg, (8 if (cg_type(cg)[1] >= 4) else 2)) for cg in KV_CGS]),
            ("main", 0, 8, [(cg, 8) for cg in range(50)]),
            ("main", 1024, 8, [(cg, 8) for cg in range(50)]),
        ]

        if p1_mode is not None:
            tsel = {"v": (CG_V,), "qk": (CG_Q, CG_K), "vs": (CG_VS,), "f": (CG_U, CG_GA), "pro": ()}[p1_mode]
            blocks = [("main", 0, 8, [(cg, 2) for cg in range(50) if cg_type(cg)[0] in tsel and cg_type(cg)[1] in (0, 5)])]
        precast = []
        if w_br is not None and p1_mode is None:
            s_wbf = P.new_dsem("wbf")
            precast = [(w_br_bf[cg], w_br[cg]) for cg in range(16)] + [(w_out_bf[cg], w_out[cg]) for cg in range(8)]
        for bidx, (src, tok0, ntile, cglist) in enumerate(blocks):
            for tt in range(ntile):
                i = cnt["x"] % 2
                cnt["x"] += 1
                if src == "main":
                    sap = x_main[tok0 + tt * 128: tok0 + (tt + 1) * 128, :]
                else:
                    sap = x_halo[tt * 128:(tt + 1) * 128, :]
                prologue_tile(sap, R_in, xs[i], R_xs[i], s_xs[i], xn[i], R_xn[i],
                              hT, R_hT[tt], tt * 128, 2 * i, (6, 7))
            for (cg, ntl) in cglist:
                typ, sub = cg_type(cg)
                si = cnt["ws"] % 2
                cnt["ws"] += 1
                P.dma("pool", wslot[si].rearrange("p a b -> p (a b)"), w_in[cg], [R_in], [R_ws[si]], s_ws[si])
                if src == "main" and cg >= 18 and cg % 2 == 0 and precast:
                    pc_out, pc_in = precast.pop(0)
                    P.dma("pool", pc_out, pc_in, [R_in], [R_wbf], s_wbf)
                ws = wslot[si]
                if typ in (CG_Q, CG_K, CG_V, CG_VS):
                    for tt in range(ntl):
                        bi = cnt["bank"] % 6
                        cnt["bank"] += 1
                        tg = (tok0 + tt * 128) // 128

                        def mm(e, tt=tt, bi=bi, ws=ws):
                            ins = None
                            for kc in range(32):
                                ins = e.matmul(bank_f32(bi), lhsT=hT[:, kc, tt * 128:(tt + 1) * 128], rhs=ws[:, kc, :],
                                               start=(kc == 0), stop=(kc == 31))
                            return ins
                        P.op("pe", mm, [R_hT[tt], R_ws[si]], [bankR[bi]])
                        flush_qk()
                        pb = bank_f32(bi)
                        if typ in (CG_Q, CG_K):
                            qi = cnt["qk"] % 2
                            cnt["qk"] += 1
                            pv4 = pb.rearrange("p (h c) -> p h c", h=4)
                            qk = qk_tm[qi]
                            cc = rope[:, tg, 0:32].unsqueeze(1).broadcast_to([128, 4, 32])
                            nsn = rope[:, tg, 32:48].unsqueeze(1).broadcast_to([128, 4, 16])
                            sn = rope[:, tg, 48:64].unsqueeze(1).broadcast_to([128, 4, 16])
                            P.op("act", lambda e, qk=qk, pv4=pv4: e.copy(out=qk[:, :, 32:128], in_=pv4[:, :, 32:128]),
                                 [bankR[bi]], [R_qk[qi]])
                            P.op("dve", lambda e, pv4=pv4, cc=cc, qi=qi: e.tensor_tensor(out=ta[qi], in0=pv4[:, :, 0:32], in1=cc, op=ALU.mult),
                                 [bankR[bi], R_rope], [R_ta[qi]])
                            P.op("dve", lambda e, pv4=pv4, nsn=nsn, qi=qi: e.tensor_tensor(out=tb_[qi][:, :, 0:16], in0=pv4[:, :, 16:32], in1=nsn, op=ALU.mult),
                                 [bankR[bi], R_rope], [R_tb[qi]])
                            P.op("dve", lambda e, pv4=pv4, sn=sn, qi=qi: e.tensor_tensor(out=tb_[qi][:, :, 16:32], in0=pv4[:, :, 0:16], in1=sn, op=ALU.mult),
                                 [bankR[bi], R_rope, R_tb[qi]], [R_tb[qi]])
                            P.op("dve", lambda e, qk=qk, qi=qi: e.tensor_tensor(out=qk[:, :, 0:32], in0=ta[qi], in1=tb_[qi], op=ALU.add),
                                 [R_ta[qi], R_tb[qi], R_qk[qi]], [R_qk[qi]])
                            def qk_tail(qi=qi, qk=qk, typ=typ, sub=sub, tok0=tok0, tt=tt):
                                tbi = 6 + (qi % 2)
                                pvT = bank_bf(tbi).rearrange("p (a b) -> p a b", a=8)[:, 0:4, :]

                                def trq(e):
                                    ins = None
                                    for h in range(4):
                                        ins = e.transpose(pvT[:, h, :], qk[:, h, :], ident)
                                    return ins
                                P.op("pe", trq, [R_qk[qi], R_cst], [bankR[tbi]])
                                sq = cnt["stq"] % 2
                                cnt["stq"] += 1
                                if sq == 0:
                                    P.op("act", lambda e: e.copy(out=stQ[sq], in_=pvT), [bankR[tbi]], [R_stQ[sq]])
                                else:
                                    P.op("dve", lambda e: e.tensor_copy(out=stQ[sq], in_=pvT), [bankR[tbi]], [R_stQ[sq]])
                                h0 = sub * 4
                                if typ == CG_Q:
                                    dst = qT_s[h0:h0 + 4, :, tok0 + tt * 128: tok0 + (tt + 1) * 128].rearrange("h d t -> d h t")
                                    P.dma("sp", dst, stQ[sq], [R_stQ[sq]], [R_q], s_stQ[sq])
                                else:
                                    c0 = PADL + tok0 + tt * 128
                                    dst = kT_s[h0:h0 + 4, :, c0:c0 + 128].rearrange("h d t -> d h t")
                                    P.dma("sp", dst, stQ[sq], [R_stQ[sq]], [R_k], s_stQ[sq])
                            pend_qk.append(qk_tail)
                            if tt == ntl - 1:
                                flush_qk()
                        elif typ == CG_V:
                            s = cnt["st"] % NST
                            cnt["st"] += 1
                            if s % 2 == 0:
                                P.op("act", lambda e, s=s, pb=pb: e.copy(out=stT[s], in_=pb), [bankR[bi]], [R_stT[s]])
                            else:
                                P.op("dve", lambda e, s=s, pb=pb: e.tensor_copy(out=stT[s], in_=pb), [bankR[bi]], [R_stT[s]])
                            r0 = PADL + tok0 + tt * 128
                            P.dma("sp", v_s[r0:r0 + 128, sub * 512:(sub + 1) * 512], stT[s], [R_stT[s]], [R_v], s_stT[s])
                        else:
                            s = cnt["st"] % NST
                            cnt["st"] += 1
                            ti = cnt["tmp"] % 2
                            cnt["tmp"] += 1
                            P.op("act", lambda e, ti=ti, pb=pb: e.activation(out=tmpf[ti], in_=pb, func=AF.Gelu), [bankR[bi]], [R_tmpf[ti]])
                            acc = ssq_vs[:, tg, sub:sub + 1]
                            P.op("act", lambda e, ti=ti, acc=acc: e.activation(out=junk, in_=tmpf[ti], func=AF.Square, accum_out=acc),
                                 [R_tmpf[ti]], [R_junk, R_ssqvs])
                            P.op("dve", lambda e, ti=ti, s=s: e.tensor_copy(out=stT[s], in_=tmpf[ti]), [R_tmpf[ti]], [R_stT[s]])
                            t0 = tok0 + tt * 128
                            P.dma("sp", gvs_s[t0:t0 + 128, sub * 512:(sub + 1) * 512], stT[s], [R_stT[s]], [R_gvs], s_stT[s])
                else:
                    func = AF.Gelu if typ == CG_U else AF.Sigmoid
                    dstT, R_d = {CG_U: (guT_s, R_gu), CG_GA: (sgaT_s, R_sga), CG_GB: (sgbT_s, R_sgb)}[typ]
                    for fb in range(4):
                        for th in range(2):
                            bi = cnt["bank"] % 6
                            cnt["bank"] += 1

                            def mm(e, fb=fb, th=th, bi=bi, ws=ws):
                                ins = None
                                for kc in range(32):
                                    ins = e.matmul(bank_f32(bi), lhsT=ws[:, kc, fb * 128:(fb + 1) * 128], rhs=hT[:, kc, th * 512:(th + 1) * 512],
                                                   start=(kc == 0), stop=(kc == 31))
                                return ins
                            P.op("pe", mm, [R_hT[th * 4 + i] for i in range(4)] + [R_ws[si]], [bankR[bi]])
                            s = cnt["st"] % NST
                            cnt["st"] += 1
                            pb = bank_f32(bi)
                            P.op("act", lambda e, s=s, pb=pb, func=func: e.activation(out=stT[s], in_=pb, func=func), [bankR[bi]], [R_stT[s]])
                            f0 = sub * 512 + fb * 128
                            t0 = tok0 + th * 512
                            P.dma("sp", dstT[f0:f0 + 128, t0:t0 + 512], stT[s], [R_stT[s]], [R_d], s_stT[s])
        P.barrier()
        if stop_after == 1:
            P.emit()
            return nc
        AR.off = PERSIST

        validT = AR.alloc([69, 128], BF16)
        R_vT = Res("validT")
        qsb = [AR.alloc([NTOK], BF16) for _ in range(2)]
        ksb = [AR.alloc([4096], BF16) for _ in range(2)]
        vsb = [AR.alloc([32, 128], BF16) for _ in range(2)]
        R_qsb = [Res("qsb%d" % i) for i in range(2)]
        R_ksb = [Res("ksb%d" % i) for i in range(2)]
        R_vsb = [Res("vsb%d" % i) for i in range(2)]
        s_qsb = [P.new_dsem("qsb%d" % i) for i in range(2)]
        s_ksb = [P.new_dsem("ksb%d" % i) for i in range(2)]
        s_vsb = [P.new_dsem("vsb%d" % i) for i in range(2)]
        acc = AR.alloc([2, NTOK], F32)
        R_acc = Res("acc")
        Pb = [AR.alloc([2, 128], BF16) for _ in range(3)]
        R_Pb = [Res("Pb%d" % i) for i in range(3)]
        print("phase2 arena", AR.off)

        def mk_valid(e):
            ins = None
            for idx in range(69):
                ins = e.tensor_scalar(out=validT[:, idx, :], in0=ones_bf, scalar1=validcol[:, idx:idx + 1], scalar2=None, op0=ALU.mult)
            return ins
        P.op("dve", mk_valid, [R_cst, R_valid], [R_vT])

        vt_base = (0, 17, 37)
        hcnt = 0
        it_cnt = 0
        pending = None
        for j in range(8):
            for g in range(3):
                d = DILS[g]
                L = NTOK // d
                nqt = L // 128
                nkt = nqt + 1
                h = g * 8 + j
                hi = hcnt % 2
                hcnt += 1
                P.dma("sp", qsb[hi], qT_s[h, :, :], [R_q], [R_qsb[hi]], s_qsb[hi])
                P.dma("sp", ksb[hi], kT_s[h, :, :], [R_k], [R_ksb[hi]], s_ksb[hi])
                voff = (PADL - 64 * d) * 3072 + h * 128
                vsrc = bass.AP(v_s.tensor, voff, [[d * 3072, 128], [3072, d], [128 * d * 3072, nkt], [1, 128]])
                vdst = vsb[hi][:, 0:d * nkt, :].rearrange("p (r k) c -> p r k c", r=d)
                P.dma("sp", vdst, vsrc, [R_v], [R_vsb[hi]], s_vsb[hi])
                for r in range(d):
                    for qt in range(nqt):
                        sb = it_cnt % 2
                        ob = 2 + it_cnt % 2
                        pbi = it_cnt % 3
                        it_cnt += 1
                        Sv = bank_f32(sb)[:, 0:256].rearrange("p (a b) -> p a b", a=2)
                        Ov = bank_f32(ob)[:, 0:256].rearrange("p (a b) -> p a b", a=2)
                        q0 = r + d * 128 * qt
                        q_ap = qsb[hi][:, q0:q0 + 127 * d + 1:d]

                        def qk(e, Sv=Sv, q_ap=q_ap, hi=hi, r=r, qt=qt, d=d):
                            ins = None
                            for kk in range(2):
                                kt = qt + kk
                                k0 = PADL + r + d * (128 * kt - 64)
                                e.matmul(Sv[:, kk, :], lhsT=ident, rhs=mask01[:, kk, :], start=True, stop=False)
                                ins = e.matmul(Sv[:, kk, :], lhsT=ksb[hi][:, k0:k0 + 127 * d + 1:d], rhs=q_ap, start=False, stop=True)
                            return ins
                        P.op("pe", qk, [R_cst, R_ksb[hi], R_qsb[hi]], [bankR[sb]])
                        P.op("act", lambda e, Sv=Sv, pbi=pbi: e.activation(out=Pb[pbi], in_=Sv, func=AF.Exp, scale=float(SCALE)),
                             [bankR[sb]], [R_Pb[pbi]])

                        def pv(e, Ov=Ov, pbi=pbi, hi=hi, r=r, qt=qt, g=g, d=d, nkt=nkt):
                            ins = None
                            for kk in range(2):
                                kt = qt + kk
                                ins = e.matmul(Ov[:, 0, :], lhsT=vsb[hi][:, r * nkt + kt, :], rhs=Pb[pbi][:, kk, :], start=(kk == 0), stop=(kk == 1))
                            for kk in range(2):
                                kt = qt + kk
                                ins = e.matmul(Ov[:, 1, :], lhsT=validT[:, vt_base[g] + r * nkt + kt, :], rhs=Pb[pbi][:, kk, :], start=(kk == 0), stop=(kk == 1))
                            return ins
                        accv = acc[:, :, q0:q0 + 127 * d + 1:d]
                        if g == 0:
                            fin = (lambda e, accv=accv, Ov=Ov: e.tensor_copy(out=accv, in_=Ov))
                        else:
                            fin = (lambda e, accv=accv, Ov=Ov: e.tensor_tensor(out=accv, in0=accv, in1=Ov, op=ALU.add))
                        cur = (pv, [R_Pb[pbi], R_vsb[hi], R_vT], [bankR[ob]], fin, [bankR[ob], R_acc], [R_acc])
                        if pending is not None:
                            P.op("pe", pending[0], pending[1], pending[2])
                            P.op("dve", pending[3], pending[4], pending[5])
                        pending = cur
            if pending is not None:
                P.op("pe", pending[0], pending[1], pending[2])
                P.op("dve", pending[3], pending[4], pending[5])
                pending = None
            P.op("dve", lambda e: e.reciprocal(out=acc[:, 1, :], in_=acc[:, 1, :]), [R_acc], [R_acc])
            P.op("dve", lambda e, j=j: e.tensor_tensor(out=attnT[:, j, :], in0=acc[:, 0, :], in1=acc[:, 1, :], op=ALU.mult),
                 [R_acc], [R_attnT[j]])
        if debug:
            s_dbg = P.new_dsem("dbg")
            P.dma("sp", dbg["attnT"], attnT.rearrange("p a b -> p (a b)"), R_attnT, [R_dbg], s_dbg)
        P.barrier()
        if stop_after == 2:
            P.emit()
            return nc
        AR.off = PERSIST

        sguT = AR.alloc([32, 512], BF16)
        R_sguT = [Res("sguT%d" % i) for i in range(4)]
        mergedT = AR.alloc([32, 512], BF16)
        R_mT = [Res("mT%d" % i) for i in range(8)]
        WsT = AR.alloc([16, 128], BF16)
        R_WsT = Res("WsT")
        s_WsT = P.new_dsem("WsT")
        sgub = AR.alloc([16, 128], BF16)
        R_sgub = Res("sgub")
        s_sgub = P.new_dsem("sgub")
        rstd_vs = AR.alloc([16], F32)
        R_rvs = Res("rstd_vs")
        VAR35 = AR.off
        sgn_bc = AR.alloc([D], F32)
        R_sgn = Res("sgn")
        s_sgn = P.new_dsem("sgn")
        gvs = [AR.alloc([D], BF16) for _ in range(2)]
        R_gvsb = [Res("gvsb%d" % i) for i in range(2)]
        s_gvsb = [P.new_dsem("gvsb%d" % i) for i in range(2)]
        vsn = [AR.alloc([D], BF16) for _ in range(2)]
        R_vsn = [Res("vsn%d" % i) for i in range(2)]
        guT = [AR.alloc([32, 128], BF16) for _ in range(2)]
        R_guT = [Res("guT%d" % i) for i in range(2)]
        s_guT = [P.new_dsem("guT%d" % i) for i in range(2)]
        print("phase3 arena", AR.off)
        AR.off = VAR35
        wsl_flat = [AR.alloc([32 * 512], BF16) for _ in range(2)]
        wsl = [w[:, 0:40 * 256].rearrange("p (a b) -> p a b", a=40) for w in wsl_flat]
        wsl5 = [w.rearrange("p (a b) -> p a b", a=32) for w in wsl_flat]
        R_wsl = [Res("wsl%d" % i) for i in range(2)]
        s_wsl = [P.new_dsem("wsl%d" % i) for i in range(2)]
        sga = [AR.alloc([512], BF16) for _ in range(2)]
        sgb = [AR.alloc([512], BF16) for _ in range(2)]
        R_sgab = [Res("sga%d" % i) for i in range(2)]
        R_sgbb = [Res("sgb%d" % i) for i in range(2)]
        s_sgab = [P.new_dsem("sga%d" % i) for i in range(2)]
        s_sgbb = [P.new_dsem("sgb%d" % i) for i in range(2)]
        t1 = [AR.alloc([512], F32) for _ in range(2)]
        t2 = [AR.alloc([512], F32) for _ in range(2)]
        R_t1 = [Res("t1%d" % i) for i in range(2)]
        R_t2 = [Res("t2%d" % i) for i in range(2)]
        xr = [AR.alloc([512], F32) for _ in range(2)]
        R_xr = [Res("xr%d" % i) for i in range(2)]
        s_xr = [P.new_dsem("xr%d" % i) for i in range(2)]
        xo = [AR.alloc([512], F32) for _ in range(2)]
        R_xo = [Res("xo%d" % i) for i in range(2)]
        s_xo = [P.new_dsem("xo%d" % i) for i in range(2)]
        print("phase4-5 arena", AR.off)

        P.dma("pool", WsT.rearrange("p a b -> p (a b)"), sgu_wT, [R_in], [R_WsT], s_WsT)
        P.dma("pool", sgub.rearrange("p a b -> p (a b)")[0:1, :], sgu_b, [R_in], [R_sgub], s_sgub)
        P.op("dve", lambda e: e.tensor_reduce(out=rstd_vs, in_=ssq_vs, axis=mybir.AxisListType.X, op=ALU.add), [R_ssqvs], [R_rvs])
        P.op("dve", lambda e: e.tensor_scalar(out=rstd_vs, in0=rstd_vs, scalar1=1.0 / D, scalar2=EPS, op0=ALU.mult, op1=ALU.add), [R_rvs], [R_rvs])
        P.op("act", lambda e: e.activation(out=rstd_vs, in_=rstd_vs, func=AF.Sqrt), [R_rvs], [R_rvs])
        P.op("dve", lambda e: e.reciprocal(out=rstd_vs, in_=rstd_vs), [R_rvs], [R_rvs])

        c35 = {"ws": 0, "bank": 0, "ch": 0, "sg": 0, "t": 0, "x": 0}
        for b in range(4):
            tb0 = b * 512
            load_bc(n_sgu, sgn_bc, R_sgn, s_sgn)
            for c in range(4):
                ci = c35["ch"] % 2
                c35["ch"] += 1
                cidx = b * 4 + c
                t0 = tb0 + c * 128
                P.dma("sp", gvs[ci], gvs_s[t0:t0 + 128, :], [R_gvs], [R_gvsb[ci]], s_gvsb[ci])
                P.dma("sp", guT[ci], guT_s[:, t0:t0 + 128].rearrange("(cb p) t -> p cb t", p=128), [R_gu], [R_guT[ci]], s_guT[ci])
                P.op("dve", lambda e, ci=ci, cidx=cidx: e.scalar_tensor_tensor(out=vsn[ci], in0=gvs[ci], scalar=rstd_vs[:, cidx:cidx + 1], in1=sgn_bc,
                                                                                op0=ALU.mult, op1=ALU.mult),
                     [R_gvsb[ci], R_rvs, R_sgn], [R_vsn[ci]])
                for cb4 in range(8):
                    bi = c35["bank"] % 6
                    c35["bank"] += 1
                    bv = bank_f32(bi).rearrange("p (a b) -> p a b", a=4)

                    def sg(e, bv=bv, cb4=cb4, ci=ci):
                        ins = None
                        for cbi in range(4):
                            cb = cb4 * 4 + cbi
                            g = cb // 2
                            e.matmul(bv[:, cbi, :], lhsT=ones_bf[0:1, :], rhs=sgub[0:1, g, :], start=True, stop=False)
                            ins = e.matmul(bv[:, cbi, :], lhsT=vsn[ci][:, cb * 128:(cb + 1) * 128], rhs=WsT[:, g, :], start=False, stop=True)
                        return ins
                    P.op("pe", sg, [R_cst, R_sgub, R_vsn[ci], R_WsT], [bankR[bi]])
                    dst = sguT[:, cb4 * 4:(cb4 + 1) * 4, c * 128:(c + 1) * 128]
                    P.op("dve", lambda e, dst=dst, bv=bv, ci=ci, cb4=cb4: e.tensor_tensor(out=dst, in0=bv, in1=guT[ci][:, cb4 * 4:(cb4 + 1) * 4, :], op=ALU.mult),
                         [bankR[bi], R_guT[ci]], [R_sguT[c]])
            P.barrier()
            for cg in range(16):
                si = c35["ws"] % 2
                c35["ws"] += 1
                P.dma("act", wsl_flat[si][:, 0:40 * 256], w_br_bf[cg], [R_wbf], [R_wsl[si]], s_wsl[si])
                ws = wsl[si]
                for fb in range(2):
                    ba = c35["bank"] % 6
                    bb = (c35["bank"] + 1) % 6
                    c35["bank"] += 2
                    f0 = cg * 256 + fb * 128
                    gi = c35["sg"] % 2
                    c35["sg"] += 1
                    P.dma("sp", sga[gi], sgaT_s[f0:f0 + 128, tb0:tb0 + 512], [R_sga], [R_sgab[gi]], s_sgab[gi])
                    P.dma("sp", sgb[gi], sgbT_s[f0:f0 + 128, tb0:tb0 + 512], [R_sgb], [R_sgbb[gi]], s_sgbb[gi])

                    def mma(e, ba=ba, ws=ws, fb=fb, tb0=tb0):
                        ins = None
                        for kc in range(8):
                            ins = e.matmul(bank_f32(ba), lhsT=ws[:, kc, fb * 128:(fb + 1) * 128], rhs=attnT[:, kc, tb0:tb0 + 512], start=(kc == 0), stop=(kc == 7))
                        return ins

                    def mmb(e, bb=bb, ws=ws, fb=fb):
                        ins = None
                        for kc in range(32):
                            ins = e.matmul(bank_f32(bb), lhsT=ws[:, 8 + kc, fb * 128:(fb + 1) * 128], rhs=sguT[:, kc, :], start=(kc == 0), stop=(kc == 31))
                        return ins
                    P.op("pe", mma, R_attnT + [R_wsl[si]], [bankR[ba]])
                    P.op("pe", mmb, R_sguT + [R_wsl[si]], [bankR[bb]])
                    P.op("dve", lambda e, gi=gi, ba=ba: e.tensor_tensor(out=t1[gi], in0=bank_f32(ba), in1=sga[gi], op=ALU.mult),
                         [bankR[ba], R_sgab[gi]], [R_t1[gi]])
                    P.op("dve", lambda e, gi=gi, bb=bb: e.tensor_tensor(out=t2[gi], in0=bank_f32(bb), in1=sgb[gi], op=ALU.mult),
                         [bankR[bb], R_sgbb[gi]], [R_t2[gi]])
                    mdst = mergedT[:, cg * 2 + fb, :]
                    P.op("pool", lambda e, gi=gi, mdst=mdst: e.tensor_tensor(out=mdst, in0=t1[gi], in1=t2[gi], op=ALU.add),
                         [R_t1[gi], R_t2[gi]], [R_mT[cg // 2]])
            for cg in range(8):
                si = c35["ws"] % 2
                c35["ws"] += 1
                P.dma("act", wsl_flat[si], w_out_bf[cg], [R_wbf], [R_wsl[si]], s_wsl[si])
                ws = wsl5[si]
                for tt in range(4):
                    bi = c35["bank"] % 6
                    c35["bank"] += 1
                    xi = c35["x"] % 2
                    c35["x"] += 1
                    t0 = tb0 + tt * 128
                    P.dma("sp", xr[xi], x_main[t0:t0 + 128, cg * 512:(cg + 1) * 512], [R_in], [R_xr[xi]], s_xr[xi])

                    def mmo(e, bi=bi, ws=ws, tt=tt):
                        ins = None
                        for kc in range(32):
                            ins = e.matmul(bank_f32(bi), lhsT=mergedT[:, kc, tt * 128:(tt + 1) * 128], rhs=ws[:, kc, :], start=(kc == 0), stop=(kc == 31))
                        return ins
                    P.op("pe", mmo, R_mT + [R_wsl[si]], [bankR[bi]])
                    P.op("dve", lambda e, xi=xi, bi=bi: e.tensor_tensor(out=xo[xi], in0=bank_f32(bi), in1=xr[xi], op=ALU.add),
                         [bankR[bi], R_xr[xi]], [R_xo[xi]])
                    P.dma("sp", x1_s[t0:t0 + 128, cg * 512:(cg + 1) * 512], xo[xi], [R_xo[xi]], [R_x1], s_xo[xi])
            P.barrier()
            if stop_after == 3 + 0.1 * b:
                P.emit()
                return nc
        if stop_after == 5:
            P.emit()
            return nc
        AR.off = PERSIST0

        h2T = AR.alloc([32, 1024], BF16)
        R_h2T = [Res("h2T%d" % i) for i in range(8)]
        ws6 = [AR.alloc([64, 256], BF16) for _ in range(2)]
        R_ws6 = [Res("ws6%d" % i, multi=True) for i in range(2)]
        s_ws6 = [P.new_dsem("ws6%d" % i) for i in range(2)]
        xs6 = [AR.alloc([D], F32) for _ in range(2)]
        R_xs6 = [Res("xs6%d" % i) for i in range(2)]
        s_xs6 = [P.new_dsem("xs6%d" % i) for i in range(2)]
        xn6 = [AR.alloc([D], BF16) for _ in range(2)]
        R_xn6 = [Res("xn6%d" % i) for i in range(2)]
        tg6 = [AR.alloc([512], F32) for _ in range(2)]
        R_tg6 = [Res("tg6%d" % i) for i in range(2)]
        sta = [AR.alloc([512], BF16) for _ in range(3)]
        R_sta = [Res("sta%d" % i) for i in range(3)]
        s_sta = [P.new_dsem("sta%d" % i) for i in range(3)]
        print("phase6 arena", AR.off)
        P.dma("sp", gcol, n_ffn, [R_in], [R_gcol], s_gcol)
        c6 = {"ws": 0, "bank": 0, "x": 0, "t": 0, "st": 0}
        for B in range(2):
            for tt in range(8):
                i = c6["x"] % 2
                c6["x"] += 1
                t0 = B * 1024 + tt * 128
                prologue_tile(x1_s[t0:t0 + 128, :], R_x1, xs6[i], R_xs6[i], s_xs6[i], xn6[i], R_xn6[i],
                              h2T, R_h2T[tt], tt * 128, 2 * i, (6, 7))
            for cgf in range(43):
                si = c6["ws"] % 2
                c6["ws"] += 1
                P.dma("pool", ws6[si][:, 0:32, :].rearrange("p a b -> p (a b)"), w_gate[cgf], [R_in], [R_ws6[si]], s_ws6[si])
                P.dma("pool", ws6[si][:, 32:64, :].rearrange("p a b -> p (a b)"), w_up[cgf], [R_in], [R_ws6[si]], s_ws6[si])
                ws = ws6[si]
                for fb in range(2):
                    for th in range(2):
                        bg = c6["bank"] % 6
                        bu = (c6["bank"] + 1) % 6
                        c6["bank"] += 2

                        def mmg(e, bg=bg, ws=ws, fb=fb, th=th):
                            ins = None
                            for kc in range(32):
                                ins = e.matmul(bank_f32(bg), lhsT=ws[:, kc, fb * 128:(fb + 1) * 128], rhs=h2T[:, kc, th * 512:(th + 1) * 512], start=(kc == 0), stop=(kc == 31))
                            return ins

                        def mmu(e, bu=bu, ws=ws, fb=fb, th=th):
                            ins = None
                            for kc in range(32):
                                ins = e.matmul(bank_f32(bu), lhsT=ws[:, 32 + kc, fb * 128:(fb + 1) * 128], rhs=h2T[:, kc, th * 512:(th + 1) * 512], start=(kc == 0), stop=(kc == 31))
                            return ins
                        rd = [R_h2T[th * 4 + i] for i in range(4)] + [R_ws6[si]]
                        P.op("pe", mmg, rd, [bankR[bg]])
                        P.op("pe", mmu, rd, [bankR[bu]])
                        ti = c6["t"] % 2
                        c6["t"] += 1
                        s = c6["st"] % 3
                        c6["st"] += 1
                        P.op("act", lambda e, ti=ti, bg=bg: e.activation(out=tg6[ti], in_=bank_f32(bg), func=AF.Silu), [bankR[bg]], [R_tg6[ti]])
                        P.op("dve", lambda e, ti=ti, bu=bu, s=s: e.tensor_tensor(out=sta[s], in0=bank_f32(bu), in1=tg6[ti], op=ALU.mult),
                             [bankR[bu], R_tg6[ti]], [R_sta[s]])
                        f0 = cgf * 256 + fb * 128
                        c0 = B * 1024 + th * 512
                        P.dma("sp", actT_s[f0:f0 + 128, c0:c0 + 512], sta[s], [R_sta[s]], [R_act], s_sta[s])
        P.barrier()
        if stop_after == 6:
            P.emit()
            return nc
        AR.off = PERSIST0

        A7 = AR.alloc([43, 1024], BF16)
        R_A7 = [Res("A7%d" % i) for i in range(4)]
        s_A7 = [P.new_dsem("A7%d" % i) for i in range(4)]
        ws7 = [AR.alloc([43, 512], BF16) for _ in range(2)]
        R_ws7 = [Res("ws7%d" % i) for i in range(2)]
        s_ws7 = [P.new_dsem("ws7%d" % i) for i in range(2)]
        xr7 = [AR.alloc([512], F32) for _ in range(2)]
        R_xr7 = [Res("xr7%d" % i) for i in range(2)]
        s_xr7 = [P.new_dsem("xr7%d" % i) for i in range(2)]
        xo7 = [AR.alloc([512], F32) for _ in range(2)]
        R_xo7 = [Res("xo7%d" % i) for i in range(2)]
        s_xo7 = [P.new_dsem("xo7%d" % i) for i in range(2)]
        print("phase7 arena", AR.off)
        c7 = {"ws": 0, "bank": 0, "x": 0}
        for hh in range(2):
            src_s, R_src = (x1_s, R_x1) if hh == 0 else (p1_s, R_p1)
            dst_s, R_dsts = (p1_s, R_p1) if hh == 0 else (x2_s, R_x2)
            for B in range(2):
                for g4 in range(4):
                    P.dma("act", A7[:, :, g4 * 256:(g4 + 1) * 256],
                          actT_s[hh * 5504:(hh + 1) * 5504, B * 1024 + g4 * 256:B * 1024 + (g4 + 1) * 256].rearrange("(kc p) t -> p kc t", p=128),
                          [R_act], [R_A7[g4]], s_A7[g4])
                for cg in range(8):
                    si = c7["ws"] % 2
                    c7["ws"] += 1
                    P.dma("pool", ws7[si].rearrange("p a b -> p (a b)"), w_down[hh * 8 + cg], [R_in], [R_ws7[si]], s_ws7[si])
                    ws = ws7[si]
                    for tt in range(8):
                        bi = c7["bank"] % 6
                        c7["bank"] += 1
                        xi = c7["x"] % 2
                        c7["x"] += 1
                        t0 = B * 1024 + tt * 128
                        P.dma("sp", xr7[xi], src_s[t0:t0 + 128, cg * 512:(cg + 1) * 512], [R_src], [R_xr7[xi]], s_xr7[xi])

                        def mmd(e, bi=bi, ws=ws, tt=tt):
                            ins = None
                            for kc in range(43):
                                ins = e.matmul(bank_f32(bi), lhsT=A7[:, kc, tt * 128:(tt + 1) * 128], rhs=ws[:, kc, :], start=(kc == 0), stop=(kc == 42))
                            return ins
                        P.op("pe", mmd, [R_A7[tt // 2], R_ws7[si]], [bankR[bi]])
                        P.op("dve", lambda e, xi=xi, bi=bi: e.tensor_tensor(out=xo7[xi], in0=bank_f32(bi), in1=xr7[xi], op=ALU.add),
                             [bankR[bi], R_xr7[xi]], [R_xo7[xi]])
                        P.dma("sp", dst_s[t0:t0 + 128, cg * 512:(cg + 1) * 512], xo7[xi], [R_xo7[xi]], [R_dsts], s_xo7[xi])
        P.barrier()
        AR.off = PERSIST0

        fbc = AR.alloc([D], F32)
        R_fbc = Res("fbc")
        s_fbc = P.new_dsem("fbc")
        xs8 = [AR.alloc([D], F32) for _ in range(2)]
        R_xs8 = [Res("xs8%d" % i) for i in range(2)]
        s_xs8 = [P.new_dsem("xs8%d" % i) for i in range(2)]
        ys8 = [AR.alloc([D], F32) for _ in range(2)]
        R_ys8 = [Res("ys8%d" % i) for i in range(2)]
        s_ys8 = [P.new_dsem("ys8%d" % i) for i in range(2)]
        load_bc(n_fin, fbc, R_fbc, s_fbc)
        for tt in range(16):
            i = tt % 2
            P.dma("sp", xs8[i], x2_s[tt * 128:(tt + 1) * 128, :], [R_x2], [R_xs8[i]], s_xs8[i])
            ssq = small[:, 2 * i:2 * i + 1]
            rstd = small[:, 2 * i + 1:2 * i + 2]
            R_st = Res("st8")
            P.op("act", lambda e, i=i, ssq=ssq: e.activation(out=ys8[i], in_=xs8[i], func=AF.Square, accum_out=ssq), [R_xs8[i]], [R_ys8[i], R_st])
            P.op("dve", lambda e, ssq=ssq, rstd=rstd: e.tensor_scalar(out=rstd, in0=ssq, scalar1=1.0 / D, scalar2=EPS, op0=ALU.mult, op1=ALU.add), [R_st], [R_st])
            P.op("act", lambda e, rstd=rstd: e.activation(out=rstd, in_=rstd, func=AF.Sqrt), [R_st], [R_st])
            P.op("dve", lambda e, rstd=rstd: e.reciprocal(out=rstd, in_=rstd), [R_st], [R_st])
            P.op("dve", lambda e, i=i, rstd=rstd: e.scalar_tensor_tensor(out=ys8[i], in0=xs8[i], scalar=rstd, in1=fbc, op0=ALU.mult, op1=ALU.mult),
                 [R_xs8[i], R_st, R_fbc], [R_ys8[i]])
            P.dma("sp", y_out[tt * 128:(tt + 1) * 128, :], ys8[i], [R_ys8[i]], [R_y], s_ys8[i])
        P.barrier()
        P.emit()
    return nc


def _rope_table(pos):
    inv_freq = (np.float32(500000.0) ** (-np.arange(0, 32, 2, dtype=np.float32) / np.float32(32))).astype(np.float32)
    ang = (pos[:, None].astype(np.float32) * inv_freq[None, :]).astype(np.float32)
    c = np.cos(ang).astype(np.float32)
    s = np.sin(ang).astype(np.float32)
    tab = np.concatenate([c, c, -s, s], axis=1)
    return np.ascontiguousarray(tab.reshape(24, 128, 64).transpose(1, 0, 2).reshape(128, 24 * 64))


def _valid_cols(halo_valid):
    out = np.zeros((128, 69), np.float32)
    idx = 0
    i = np.arange(128)
    for g, d in enumerate(DILS):
        nkt = NTOK // d // 128 + 1
        for r in range(d):
            for kt in range(nkt):
                t = r + d * (-64 + 128 * kt + i)
                ok = (t >= 0) & ((t < NTOK) | halo_valid)
                out[:, idx] = ok.astype(np.float32)
                idx += 1
    assert idx == 69
    return out


def _consts():
    i = np.arange(128)
    ident = np.eye(128, dtype=np.float32)
    m0 = np.where(i[:, None] >= i[None, :], 0.0, MASKV).astype(np.float32)
    m1 = np.where(i[:, None] <= i[None, :], 0.0, MASKV).astype(np.float32)
    ones = np.ones((128, 128), np.float32)
    return np.ascontiguousarray(np.concatenate([ident, m0, m1, ones], axis=1))


_NC_CACHE = {}


def make_in_maps(inputs):
    f = lambda a: np.ascontiguousarray(np.asarray(a, dtype=np.float32))
    xp = f(inputs["x_prompt"])
    xsm = f(inputs["x_sample"])
    def tile_w(w, kc, nw):
        K, N = w.shape
        assert K == kc * 128 and N % nw == 0
        return np.ascontiguousarray(w.reshape(kc, 128, N // nw, nw).transpose(2, 1, 0, 3)).reshape(N // nw, 128, kc * nw)
    wd = f(inputs["w_down"])[0]
    shared = {
        "w_in": tile_w(f(inputs["w_in"])[0], 32, 512), "w_branch": tile_w(f(inputs["w_branch"])[0], 40, 256),
        "w_out": tile_w(f(inputs["w_out"])[0], 32, 512),
        "w_gate": tile_w(f(inputs["w_gate"])[0], 32, 256), "w_up": tile_w(f(inputs["w_up"])[0], 32, 256),
        "w_down": np.concatenate([tile_w(wd[0:5504], 43, 512), tile_w(wd[5504:11008], 43, 512)], axis=0),
        "attn_norm": np.ascontiguousarray(f(inputs["attn_norm"])[0].reshape(32, 128).T),
        "ffn_norm": np.ascontiguousarray(f(inputs["ffn_norm"])[0].reshape(32, 128).T),
        "sgu_norm": f(inputs["sgu_norm"])[0], "final_norm": f(inputs["final_norm"]),
        "consts": _consts(),
    }
    sw = f(inputs["sgu_w"])[0]
    sb = f(inputs["sgu_b"])[0]
    wT_fwd = np.ascontiguousarray(sw.transpose(2, 0, 1).reshape(128, 16 * 128))
    wT_rev = np.ascontiguousarray(sw[:, ::-1, ::-1].transpose(2, 0, 1).reshape(128, 16 * 128))
    b_fwd = np.ascontiguousarray(sb.reshape(1, 16 * 128))
    b_rev = np.ascontiguousarray(sb[:, ::-1].reshape(1, 16 * 128))
    zeros_halo = np.zeros((HALO, D), np.float32)
    maps = []
    for c in range(8):
        m = dict(shared)
        if c < 4:
            m["x_main"] = xp[c]
            m["x_halo"] = zeros_halo
            pos = np.arange(3072, dtype=np.float32)
            hv = False
            rev = False
        else:
            sq = (c - 4) // 2
            rev = (c - 4) % 2 == 1
            xx = xsm[sq][::-1] if rev else xsm[sq]
            m["x_main"] = np.ascontiguousarray(xx[0:NTOK])
            m["x_halo"] = np.ascontiguousarray(xx[NTOK:NTOK + HALO])
            pos = np.arange(3072, dtype=np.float32)
            if rev:
                pos = (4095.0 - pos).astype(np.float32)
            hv = True
        m["rope"] = _rope_table(pos)
        m["validcol"] = _valid_cols(hv)
        m["sgu_wT"] = wT_rev if rev else wT_fwd
        m["sgu_b"] = b_rev if rev else b_fwd
        maps.append(m)
    return maps


def assemble(results):
    yp = np.stack([np.asarray(results[c]["y"], dtype=np.float32) for c in range(4)], axis=0)
    ys = []
    for sq in range(2):
        a = np.asarray(results[4 + 2 * sq]["y"], dtype=np.float32)
        b = np.asarray(results[5 + 2 * sq]["y"], dtype=np.float32)[::-1]
        ys.append(np.concatenate([a, b], axis=0))
    return yp, np.stack(ys, axis=0)


def kernel(**inputs):
    if "nc" not in _NC_CACHE:
        _NC_CACHE["nc"] = build_program(False)
    nc = _NC_CACHE["nc"]
    maps = make_in_maps(inputs)
    res = run_bass_kernel_spmd(nc, maps, core_ids=list(range(8)))
    yp, ys = assemble(res.results)
    return (np.ascontiguousarray(yp), np.ascontiguousarray(ys))
```

```python
import contextlib
import numpy as np
import concourse.bass as bass
import concourse.mybir as mybir
from concourse.bass_utils import run_bass_kernel_spmd

F32 = mybir.dt.float32
BF16 = mybir.dt.bfloat16
AF = mybir.ActivationFunctionType
ALU = mybir.AluOpType

D = 4096
NTOK = 2048
HALO = 1024
PADL = 1024
DFF = 11008
INW = 25600
EPS = 1e-6
SCALE = 1.0 / np.sqrt(128.0)
MASKV = -30000.0
DILS = (1, 4, 16)
SAME_SYNC = True
ARENA_BYTES = 207 * 1024

CG_Q, CG_K, CG_V, CG_U, CG_VS, CG_GA, CG_GB = range(7)


def cg_type(cg):
    if cg < 6:
        return CG_Q, cg
    if cg < 12:
        return CG_K, cg - 6
    if cg < 18:
        return CG_V, cg - 12
    if cg < 26:
        return CG_U, cg - 18
    if cg < 34:
        return CG_VS, cg - 26
    if cg < 42:
        return CG_GA, cg - 34
    return CG_GB, cg - 42


class Res:
    __slots__ = ("name", "w", "r", "multi", "excl")

    def __init__(self, name, multi=False, excl=False):
        self.name = name
        self.w = {}
        self.r = {}
        self.multi = multi
        self.excl = excl


class Prog:
    ENGS = ("pe", "act", "dve", "pool", "sp")

    def __init__(self, nc, es):
        self.nc = nc
        self.es = es
        self.q = {e: [] for e in self.ENGS}
        self.tick = {e: 0 for e in self.ENGS}
        self.waited = {e: {} for e in self.ENGS}
        self.esem = {e: es.enter_context(nc.semaphore("sem_" + e)) for e in self.ENGS}
        self.dsems = []

    def new_dsem(self, name):
        h = self.es.enter_context(self.nc.semaphore("d_" + name))
        self.dsems.append([h, 0])
        return len(self.dsems) - 1

    def _deps(self, eng, reads, writes):
        deps = {}
        for r in reads:
            for k, v in r.w.items():
                if deps.get(k, 0) < v:
                    deps[k] = v
        for w in writes:
            for k, v in w.r.items():
                if deps.get(k, 0) < v:
                    deps[k] = v
            if not w.multi:
                for k, v in w.w.items():
                    if deps.get(k, 0) < v:
                        deps[k] = v
        wt = self.waited[eng]
        for k, v in deps.items():
            if k == ("e", eng) and not SAME_SYNC:
                continue
            if wt.get(k, 0) < v:
                wt[k] = v
                self.q[eng].append(("w", k, v))

    @staticmethod
    def _mark(k, v, reads, writes):
        for w in writes:
            if w.multi:
                if w.w.get(k, 0) < v:
                    w.w[k] = v
            else:
                w.w = {k: v}
                w.r = {}
        for r in reads:
            if r.r.get(k, 0) < v:
                r.r[k] = v

    def op(self, eng, fn, reads=(), writes=()):
        ex = [r for r in reads if r.excl]
        if ex:
            reads = [r for r in reads if not r.excl]
            writes = list(writes) + [r for r in ex if r not in writes]
        self._deps(eng, reads, writes)
        self.tick[eng] += 1
        self.q[eng].append(("o", fn))
        self._mark(("e", eng), self.tick[eng], reads, writes)

    def dma(self, qeng, out, in_, reads, writes, sem):
        self._deps(qeng, reads, writes)
        self.dsems[sem][1] += 16
        self.q[qeng].append(("d", out, in_, sem))
        self._mark(("d", sem), self.dsems[sem][1], reads, writes)

    def barrier(self):
        for e in self.ENGS:
            wt = self.waited[e]
            for f in self.ENGS:
                k = ("e", f)
                if f != e and wt.get(k, 0) < self.tick[f]:
                    wt[k] = self.tick[f]
                    self.q[e].append(("w", k, self.tick[f]))
            for i, (_, c) in enumerate(self.dsems):
                k = ("d", i)
                if wt.get(k, 0) < c:
                    wt[k] = c
                    self.q[e].append(("w", k, c))

    def emit(self):
        nc = self.nc

        def body_for(name):
            def body(e):
                for it in self.q[name]:
                    if it[0] == "w":
                        k, v = it[1], it[2]
                        sem = self.esem[k[1]] if k[0] == "e" else self.dsems[k[1]][0]
                        e.wait_ge(sem, v)
                    elif it[0] == "o":
                        it[1](e).then_inc(self.esem[name], 1)
                    else:
                        e.dma_start(out=it[1], in_=it[2]).then_inc(self.dsems[it[3]][0], 16)
            return body

        with nc.Block() as block:
            block.tensor(body_for("pe"))
            block.scalar(body_for("act"))
            block.vector(body_for("dve"))
            block.gpsimd(body_for("pool"))
            block.sync(body_for("sp"))


class Arena:
    def __init__(self, t, nbytes):
        self.t = t
        self.n = nbytes
        self.off = 0

    def alloc(self, shape, dt):
        esz = 2 if dt == BF16 else 4
        n = int(np.prod(shape)) * esz
        n = (n + 63) // 64 * 64
        assert self.off + n <= self.n, ("arena overflow", self.off, n)
        a = self.t[:, self.off // 2:(self.off + n) // 2]
        if dt == F32:
            a = a.bitcast(F32)
        a = a[:, 0:int(np.prod(shape))]
        self.off += n
        if len(shape) == 2:
            a = a.rearrange("p (a b) -> p a b", a=shape[0])
        elif len(shape) == 3:
            a = a.rearrange("p (a b c) -> p a b c", a=shape[0], b=shape[1])
        return a


def build_program(debug=False, stop_after=99, p1_mode=None):
    nc = bass.Bass("TRN2", target_bir_lowering=False)
    kin = "ExternalInput"
    skind = "ExternalOutput" if debug else "Internal"

    def din(name, shape, dt=F32):
        return nc.dram_tensor(name, list(shape), dt, kind=kin).ap()

    x_main = din("x_main", [NTOK, D])
    x_halo = din("x_halo", [HALO, D])
    w_in = din("w_in", [50, 128, 32 * 512])
    if stop_after <= 2:
        w_br = w_out = w_gate = w_up = w_down = None
    else:
        w_br = din("w_branch", [16, 128, 40 * 256])
        w_out = din("w_out", [8, 128, 32 * 512])
        w_gate = din("w_gate", [43, 128, 32 * 256])
        w_up = din("w_up", [43, 128, 32 * 256])
        w_down = din("w_down", [16, 128, 43 * 512])
    n_attn = din("attn_norm", [128, 32])
    n_sgu = din("sgu_norm", [D])
    n_ffn = din("ffn_norm", [128, 32])
    n_fin = din("final_norm", [D])
    sgu_wT = din("sgu_wT", [128, 16 * 128])
    sgu_b = din("sgu_b", [1, 16 * 128])
    rope_in = din("rope", [128, 24 * 64])
    valid_in = din("validcol", [128, 69])
    consts_in = din("consts", [128, 4 * 128])
    y_out = nc.dram_tensor("y", [NTOK, D], F32, kind="ExternalOutput").ap()

    def dscr(name, shape, dt):
        return nc.dram_tensor(name, list(shape), dt, kind=skind).ap()

    qT_s = dscr("qT_s", [24, 128, NTOK], BF16)
    kT_s = dscr("kT_s", [24, 128, 4096], BF16)
    v_s = dscr("v_s", [4096, 3072], BF16)
    guT_s = dscr("guT_s", [D, NTOK], BF16)
    gvs_s = dscr("gvs_s", [NTOK, D], BF16)
    sgaT_s = dscr("sgaT_s", [D, NTOK], BF16)
    sgbT_s = dscr("sgbT_s", [D, NTOK], BF16)
    x1_s = dscr("x1_s", [NTOK, D], F32)
    actT_s = dscr("actT_s", [DFF, NTOK], BF16)
    p1_s = dscr("p1_s", [NTOK, D], F32)
    x2_s = dscr("x2_s", [NTOK, D], F32)
    w_br_bf = dscr("w_br_bf", [16, 128, 40 * 256], BF16)
    w_out_bf = dscr("w_out_bf", [8, 128, 32 * 512], BF16)
    dbg = {}
    if debug:
        dbg["attnT"] = dscr("attnT_d", [128, 8 * NTOK], BF16)

    with contextlib.ExitStack() as es:
        P = Prog(nc, es)
        arena_t = es.enter_context(nc.sbuf_tensor("arena", [128, ARENA_BYTES // 2], BF16))
        AR = Arena(arena_t, ARENA_BYTES)
        banks = [es.enter_context(nc.psum_tensor("ps%d" % i, [128, 512], F32)) for i in range(8)]
        bankR = [Res("bank%d" % i, excl=True) for i in range(8)]

        def bank_f32(i):
            return banks[i][:, :]

        def bank_bf(i):
            return banks[i][:, :].bitcast(BF16)

        R_in = Res("inputs", multi=True)
        R_q = Res("qT_s", multi=True)
        R_k = Res("kT_s", multi=True)
        R_v = Res("v_s", multi=True)
        R_gu = Res("guT_s", multi=True)
        R_gvs = Res("gvs_s", multi=True)
        R_sga = Res("sgaT_s", multi=True)
        R_sgb = Res("sgbT_s", multi=True)
        R_x1 = Res("x1_s", multi=True)
        R_act = Res("actT_s", multi=True)
        R_p1 = Res("p1_s", multi=True)
        R_x2 = Res("x2_s", multi=True)
        R_y = Res("y", multi=True)
        R_dbg = Res("dbg", multi=True)
        R_wbf = Res("wbf", multi=True)

        cst = AR.alloc([4, 128], BF16)
        ident = cst[:, 0, :]
        mask01 = cst[:, 1:3, :]
        ones_bf = cst[:, 3, :]
        rope = AR.alloc([24, 64], F32)
        validcol = AR.alloc([69], F32)
        ssq_vs = AR.alloc([16, 8], F32)
        small = AR.alloc([64], F32)
        gcol = AR.alloc([32], F32)
        R_gcol = Res("gcol")
        s_gcol = P.new_dsem("gcol")
        zt = AR.alloc([2048], BF16)
        PERSIST0 = AR.off
        attnT = AR.alloc([8, NTOK], BF16)
        R_cst = Res("cst")
        R_rope = Res("rope")
        R_valid = Res("validcol")
        R_ssqvs = Res("ssq_vs", multi=True)
        R_attnT = [Res("attnT%d" % j) for j in range(8)]
        PERSIST = AR.off

        s_misc = P.new_dsem("misc")
        P.dma("pool", cst.rearrange("p a b -> p (a b)"), consts_in, [R_in], [R_cst], s_misc)
        s_misc2 = P.new_dsem("misc2")
        P.dma("sp", rope.rearrange("p a b -> p (a b)"), rope_in, [R_in], [R_rope], s_misc2)
        s_misc3 = P.new_dsem("misc3")
        P.dma("sp", validcol, valid_in, [R_in], [R_valid], s_misc3)

        R_zt = Res("zt")
        P.op("dve", lambda e: e.memset(zt, 0.0), [], [R_zt])
        s_z = P.new_dsem("zero")
        for h0 in range(0, 24, 2):
            P.dma("sp", kT_s[h0:h0 + 2, :, 0:PADL].rearrange("h d c -> d h c"),
                  zt.rearrange("p (h c) -> p h c", h=2), [R_zt], [R_k], s_z)
        for r0 in range(0, PADL, 128):
            P.dma("sp", v_s[r0:r0 + 128, 0:2048], zt[:, 0:2048], [R_zt], [R_v], s_z)
            P.dma("sp", v_s[r0:r0 + 128, 2048:3072], zt[:, 0:1024], [R_zt], [R_v], s_z)
        if stop_after == 0:
            P.barrier()
            P.emit()
            return nc

        def prologue_tile(src_ap, R_src, xs, R_xs, s_xs, xn, R_xn, dstT, R_dst, tcol, stat_col, tb):
            P.dma("sp", xs, src_ap, [R_src], [R_xs], s_xs)
            ssq = small[:, stat_col:stat_col + 1]
            rstd = small[:, stat_col + 1:stat_col + 2]
            R_st = Res("st")
            P.op("act", lambda e: e.activation(out=xn, in_=xs, func=AF.Square, accum_out=ssq), [R_xs], [R_xn, R_st])
            P.op("dve", lambda e: e.tensor_scalar(out=rstd, in0=ssq, scalar1=1.0 / D, scalar2=EPS, op0=ALU.mult, op1=ALU.add), [R_st], [R_st])
            P.op("act", lambda e: e.activation(out=rstd, in_=rstd, func=AF.Sqrt), [R_st], [R_st])
            P.op("dve", lambda e: e.reciprocal(out=rstd, in_=rstd), [R_st], [R_st])
            P.op("act", lambda e: e.activation(out=xn, in_=xs, func=AF.Copy, scale=rstd), [R_xs, R_st], [R_xn])
            for q4 in range(4):
                bi = tb[q4 % 2]
                pv = bank_bf(bi).rearrange("p (a b) -> p a b", a=8)

                def tr(e, q4=q4, pv=pv):
                    ins = None
                    for i in range(8):
                        kc = q4 * 8 + i
                        ins = e.transpose(pv[:, i, :], xn[:, kc * 128:(kc + 1) * 128], ident)
                    return ins
                P.op("pe", tr, [R_xn, R_cst], [bankR[bi]])
                dst = dstT[:, q4 * 8:(q4 + 1) * 8, tcol:tcol + 128]
                gb = gcol[:, q4 * 8:(q4 + 1) * 8].unsqueeze(2).broadcast_to([128, 8, 128])
                P.op("dve", lambda e, dst=dst, pv=pv, gb=gb: e.tensor_tensor(out=dst, in0=pv, in1=gb, op=ALU.mult),
                     [bankR[bi], R_gcol], [R_dst])

        def load_bc(vec_ap, dst, R_dst, sem):
            P.dma("sp", dst, vec_ap.partition_broadcast(128), [R_in], [R_dst], sem)

        AR.off = PERSIST0
        hT = AR.alloc([32, 1024], BF16)
        wslot = [AR.alloc([32, 512], BF16) for _ in range(2)]
        R_ws = [Res("ws%d" % i) for i in range(2)]
        s_ws = [P.new_dsem("ws%d" % i) for i in range(2)]
        xs = [AR.alloc([D], F32) for _ in range(2)]
        R_xs = [Res("xs%d" % i) for i in range(2)]
        s_xs = [P.new_dsem("xs%d" % i) for i in range(2)]
        xn = [AR.alloc([D], BF16) for _ in range(2)]
        R_xn = [Res("xn%d" % i) for i in range(2)]
        R_hT = [Res("hT%d" % i) for i in range(8)]
        NST = 3
        stT = [AR.alloc([512], BF16) for _ in range(NST)]
        R_stT = [Res("stT%d" % i) for i in range(NST)]
        s_stT = [P.new_dsem("stT%d" % i) for i in range(NST)]
        stQ = [AR.alloc([4, 128], BF16) for _ in range(2)]
        R_stQ = [Res("stQ%d" % i) for i in range(2)]
        s_stQ = [P.new_dsem("stQ%d" % i) for i in range(2)]
        qk_tm = [AR.alloc([4, 128], BF16) for _ in range(2)]
        R_qk = [Res("qk%d" % i) for i in range(2)]
        ta = [AR.alloc([4, 32], F32) for _ in range(2)]
        tb_ = [AR.alloc([4, 32], F32) for _ in range(2)]
        R_ta = [Res("ta%d" % i) for i in range(2)]
        R_tb = [Res("tb%d" % i) for i in range(2)]
        tmpf = [AR.alloc([512], F32) for _ in range(2)]
        R_tmpf = [Res("tmpf%d" % i) for i in range(2)]
        junk = AR.alloc([512], BF16)
        R_junk = Res("junk", multi=True)
        print("phase1 arena", AR.off)

        P.dma("sp", gcol, n_attn, [R_in], [R_gcol], s_gcol)
        P.op("dve", lambda e: e.memset(ssq_vs.rearrange("p a b -> p (a b)"), 0.0), [], [R_ssqvs])

        cnt = {"ws": 0, "bank": 0, "st": 0, "stq": 0, "qk": 0, "tmp": 0, "x": 0}
        pend_qk = []

        def flush_qk():
            while pend_qk:
                pend_qk.pop(0)()
        KV_CGS = list(range(6, 18))
        blocks = [
            ("halo", NTOK, 8, [(cg, (8 if (cg_type(cg)[1] >= 4) else 2)) for cg in (6, 10, 7, 8, 11, 9, 12, 16, 13, 14, 17, 15)]),
            ("main", 0, 8, [(cg, 8) for cg in range(50)]),
            ("main", 1024, 8, [(cg, 8) for cg in range(50)]),
        ]

        if p1_mode is not None:
            tsel = {"v": (CG_V,), "qk": (CG_Q, CG_K), "vs": (CG_VS,), "f": (CG_U, CG_GA), "pro": ()}[p1_mode]
            blocks = [("main", 0, 8, [(cg, 2) for cg in range(50) if cg_type(cg)[0] in tsel and cg_type(cg)[1] in (0, 5)])]
        precast = []
        if w_br is not None and p1_mode is None:
            s_wbf = P.new_dsem("wbf")
            precast = [(w_br_bf[cg], w_br[cg]) for cg in range(16)] + [(w_out_bf[cg], w_out[cg]) for cg in range(8)]
        for bidx, (src, tok0, ntile, cglist) in enumerate(blocks):
            for tt in range(ntile):
                i = cnt["x"] % 2
                cnt["x"] += 1
                if src == "main":
                    sap = x_main[tok0 + tt * 128: tok0 + (tt + 1) * 128, :]
                else:
                    sap = x_halo[tt * 128:(tt + 1) * 128, :]
                prologue_tile(sap, R_in, xs[i], R_xs[i], s_xs[i], xn[i], R_xn[i],
                              hT, R_hT[tt], tt * 128, 2 * i, (6, 7))
            for (cg, ntl) in cglist:
                typ, sub = cg_type(cg)
                si = cnt["ws"] % 2
                cnt["ws"] += 1
                P.dma("pool", wslot[si].rearrange("p a b -> p (a b)"), w_in[cg], [R_in], [R_ws[si]], s_ws[si])
                if src == "main" and cg >= 18 and cg % 2 == 0 and precast:
                    pc_out, pc_in = precast.pop(0)
                    P.dma("pool", pc_out, pc_in, [R_in], [R_wbf], s_wbf)
                ws = wslot[si]
                if typ in (CG_Q, CG_K, CG_V, CG_VS):
                    for tt in range(ntl):
                        bi = cnt["bank"] % 6
                        cnt["bank"] += 1
                        tg = (tok0 + tt * 128) // 128

                        def mm(e, tt=tt, bi=bi, ws=ws):
                            ins = None
                            for kc in range(32):
                                ins = e.matmul(bank_f32(bi), lhsT=hT[:, kc, tt * 128:(tt + 1) * 128], rhs=ws[:, kc, :],
                                               start=(kc == 0), stop=(kc == 31))
                            return ins
                        P.op("pe", mm, [R_hT[tt], R_ws[si]], [bankR[bi]])
                        flush_qk()
                        pb = bank_f32(bi)
                        if typ in (CG_Q, CG_K):
                            qi = cnt["qk"] % 2
                            cnt["qk"] += 1
                            pv4 = pb.rearrange("p (h c) -> p h c", h=4)
                            qk = qk_tm[qi]
                            cc = rope[:, tg, 0:32].unsqueeze(1).broadcast_to([128, 4, 32])
                            nsn = rope[:, tg, 32:48].unsqueeze(1).broadcast_to([128, 4, 16])
                            sn = rope[:, tg, 48:64].unsqueeze(1).broadcast_to([128, 4, 16])
                            P.op("act", lambda e, qk=qk, pv4=pv4: e.copy(out=qk[:, :, 32:128], in_=pv4[:, :, 32:128]),
                                 [bankR[bi]], [R_qk[qi]])
                            P.op("dve", lambda e, pv4=pv4, cc=cc, qi=qi: e.tensor_tensor(out=ta[qi], in0=pv4[:, :, 0:32], in1=cc, op=ALU.mult),
                                 [bankR[bi], R_rope], [R_ta[qi]])
                            P.op("dve", lambda e, pv4=pv4, nsn=nsn, qi=qi: e.tensor_tensor(out=tb_[qi][:, :, 0:16], in0=pv4[:, :, 16:32], in1=nsn, op=ALU.mult),
                                 [bankR[bi], R_rope], [R_tb[qi]])
                            P.op("dve", lambda e, pv4=pv4, sn=sn, qi=qi: e.tensor_tensor(out=tb_[qi][:, :, 16:32], in0=pv4[:, :, 0:16], in1=sn, op=ALU.mult),
                                 [bankR[bi], R_rope, R_tb[qi]], [R_tb[qi]])
                            P.op("dve", lambda e, qk=qk, qi=qi: e.tensor_tensor(out=qk[:, :, 0:32], in0=ta[qi], in1=tb_[qi], op=ALU.add),
                                 [R_ta[qi], R_tb[qi], R_qk[qi]], [R_qk[qi]])
                            def qk_tail(qi=qi, qk=qk, typ=typ, sub=sub, tok0=tok0, tt=tt):
                                tbi = 6 + (qi % 2)
                                pvT = bank_bf(tbi).rearrange("p (a b) -> p a b", a=8)[:, 0:4, :]

                                def trq(e):
                                    ins = None
                                    for h in range(4):
                                        ins = e.transpose(pvT[:, h, :], qk[:, h, :], ident)
                                    return ins
                                P.op("pe", trq, [R_qk[qi], R_cst], [bankR[tbi]])
                                sq = cnt["stq"] % 2
                                cnt["stq"] += 1
                                if sq == 0:
                                    P.op("act", lambda e: e.copy(out=stQ[sq], in_=pvT), [bankR[tbi]], [R_stQ[sq]])
                                else:
                                    P.op("dve", lambda e: e.tensor_copy(out=stQ[sq], in_=pvT), [bankR[tbi]], [R_stQ[sq]])
                                h0 = sub * 4
                                if typ == CG_Q:
                                    dst = qT_s[h0:h0 + 4, :, tok0 + tt * 128: tok0 + (tt + 1) * 128].rearrange("h d t -> d h t")
                                    P.dma("sp", dst, stQ[sq], [R_stQ[sq]], [R_q], s_stQ[sq])
                                else:
                                    c0 = PADL + tok0 + tt * 128
                                    dst = kT_s[h0:h0 + 4, :, c0:c0 + 128].rearrange("h d t -> d h t")
                                    P.dma("sp", dst, stQ[sq], [R_stQ[sq]], [R_k], s_stQ[sq])
                            pend_qk.append(qk_tail)
                            if tt == ntl - 1:
                                flush_qk()
                        elif typ == CG_V:
                            s = cnt["st"] % NST
                            cnt["st"] += 1
                            if s % 2 == 0:
                                P.op("act", lambda e, s=s, pb=pb: e.copy(out=stT[s], in_=pb), [bankR[bi]], [R_stT[s]])
                            else:
                                P.op("dve", lambda e, s=s, pb=pb: e.tensor_copy(out=stT[s], in_=pb), [bankR[bi]], [R_stT[s]])
                            r0 = PADL + tok0 + tt * 128
                            P.dma("sp", v_s[r0:r0 + 128, sub * 512:(sub + 1) * 512], stT[s], [R_stT[s]], [R_v], s_stT[s])
                        else:
                            s = cnt["st"] % NST
                            cnt["st"] += 1
                            ti = cnt["tmp"] % 2
                            cnt["tmp"] += 1
                            P.op("act", lambda e, ti=ti, pb=pb: e.activation(out=tmpf[ti], in_=pb, func=AF.Gelu), [bankR[bi]], [R_tmpf[ti]])
                            acc = ssq_vs[:, tg, sub:sub + 1]
                            P.op("act", lambda e, ti=ti, acc=acc: e.activation(out=junk, in_=tmpf[ti], func=AF.Square, accum_out=acc),
                                 [R_tmpf[ti]], [R_junk, R_ssqvs])
                            P.op("dve", lambda e, ti=ti, s=s: e.tensor_copy(out=stT[s], in_=tmpf[ti]), [R_tmpf[ti]], [R_stT[s]])
                            t0 = tok0 + tt * 128
                            P.dma("sp", gvs_s[t0:t0 + 128, sub * 512:(sub + 1) * 512], stT[s], [R_stT[s]], [R_gvs], s_stT[s])
                else:
                    func = AF.Gelu if typ == CG_U else AF.Sigmoid
                    dstT, R_d = {CG_U: (guT_s, R_gu), CG_GA: (sgaT_s, R_sga), CG_GB: (sgbT_s, R_sgb)}[typ]
                    for fb in range(4):
                        for th in range(2):
                            bi = cnt["bank"] % 6
                            cnt["bank"] += 1

                            def mm(e, fb=fb, th=th, bi=bi, ws=ws):
                                ins = None
                                for kc in range(32):
                                    ins = e.matmul(bank_f32(bi), lhsT=ws[:, kc, fb * 128:(fb + 1) * 128], rhs=hT[:, kc, th * 512:(th + 1) * 512],
                                                   start=(kc == 0), stop=(kc == 31))
                                return ins
                            P.op("pe", mm, [R_hT[th * 4 + i] for i in range(4)] + [R_ws[si]], [bankR[bi]])
                            s = cnt["st"] % NST
                            cnt["st"] += 1
                            pb = bank_f32(bi)
                            P.op("act", lambda e, s=s, pb=pb, func=func: e.activation(out=stT[s], in_=pb, func=func), [bankR[bi]], [R_stT[s]])
                            f0 = sub * 512 + fb * 128
                            t0 = tok0 + th * 512
                            P.dma("sp", dstT[f0:f0 + 128, t0:t0 + 512], stT[s], [R_stT[s]], [R_d], s_stT[s])
        P.barrier()
        if stop_after == 1:
            P.emit()
            return nc
        AR.off = PERSIST

        validT = AR.alloc([69, 128], BF16)
        R_vT = Res("validT")
        qsb = [AR.alloc([NTOK], BF16) for _ in range(2)]
        ksb = [AR.alloc([4096], BF16) for _ in range(2)]
        vsb = [AR.alloc([32, 128], BF16) for _ in range(2)]
        R_qsb = [Res("qsb%d" % i) for i in range(2)]
        R_ksb = [Res("ksb%d" % i) for i in range(2)]
        R_vsb = [Res("vsb%d" % i) for i in range(2)]
        s_qsb = [P.new_dsem("qsb%d" % i) for i in range(2)]
        s_ksb = [P.new_dsem("ksb%d" % i) for i in range(2)]
        s_vsb = [P.new_dsem("vsb%d" % i) for i in range(2)]
        acc = AR.alloc([2, NTOK], F32)
        R_acc = Res("acc")
        Pb = [AR.alloc([2, 128], BF16) for _ in range(3)]
        R_Pb = [Res("Pb%d" % i) for i in range(3)]
        print("phase2 arena", AR.off)

        def mk_valid(e):
            ins = None
            for idx in range(69):
                ins = e.tensor_scalar(out=validT[:, idx, :], in0=ones_bf, scalar1=validcol[:, idx:idx + 1], scalar2=None, op0=ALU.mult)
            return ins
        P.op("dve", mk_valid, [R_cst, R_valid], [R_vT])

        vt_base = (0, 17, 37)
        hcnt = 0
        it_cnt = 0
        pending = None
        for j in range(8):
            for g in range(3):
                d = DILS[g]
                L = NTOK // d
                nqt = L // 128
                nkt = nqt + 1
                h = g * 8 + j
                hi = hcnt % 2
                hcnt += 1
                P.dma("sp", qsb[hi], qT_s[h, :, :], [R_q], [R_qsb[hi]], s_qsb[hi])
                P.dma("sp", ksb[hi], kT_s[h, :, :], [R_k], [R_ksb[hi]], s_ksb[hi])
                voff = (PADL - 64 * d) * 3072 + h * 128
                vsrc = bass.AP(v_s.tensor, voff, [[d * 3072, 128], [3072, d], [128 * d * 3072, nkt], [1, 128]])
                vdst = vsb[hi][:, 0:d * nkt, :].rearrange("p (r k) c -> p r k c", r=d)
                P.dma("sp", vdst, vsrc, [R_v], [R_vsb[hi]], s_vsb[hi])
                for r in range(d):
                    for qt in range(nqt):
                        sb = it_cnt % 2
                        ob = 2 + it_cnt % 2
                        pbi = it_cnt % 3
                        it_cnt += 1
                        Sv = bank_f32(sb)[:, 0:256].rearrange("p (a b) -> p a b", a=2)
                        Ov = bank_f32(ob)[:, 0:256].rearrange("p (a b) -> p a b", a=2)
                        q0 = r + d * 128 * qt
                        q_ap = qsb[hi][:, q0:q0 + 127 * d + 1:d]

                        def qk(e, Sv=Sv, q_ap=q_ap, hi=hi, r=r, qt=qt, d=d):
                            ins = None
                            for kk in range(2):
                                kt = qt + kk
                                k0 = PADL + r + d * (128 * kt - 64)
                                e.matmul(Sv[:, kk, :], lhsT=ident, rhs=mask01[:, kk, :], start=True, stop=False)
                                ins = e.matmul(Sv[:, kk, :], lhsT=ksb[hi][:, k0:k0 + 127 * d + 1:d], rhs=q_ap, start=False, stop=True)
                            return ins
                        P.op("pe", qk, [R_cst, R_ksb[hi], R_qsb[hi]], [bankR[sb]])
                        P.op("act", lambda e, Sv=Sv, pbi=pbi: e.activation(out=Pb[pbi], in_=Sv, func=AF.Exp, scale=float(SCALE)),
                             [bankR[sb]], [R_Pb[pbi]])

                        def pv(e, Ov=Ov, pbi=pbi, hi=hi, r=r, qt=qt, g=g, d=d, nkt=nkt):
                            ins = None
                            for kk in range(2):
                                kt = qt + kk
                                ins = e.matmul(Ov[:, 0, :], lhsT=vsb[hi][:, r * nkt + kt, :], rhs=Pb[pbi][:, kk, :], start=(kk == 0), stop=(kk == 1))
                            for kk in range(2):
                                kt = qt + kk
                                ins = e.matmul(Ov[:, 1, :], lhsT=validT[:, vt_base[g] + r * nkt + kt, :], rhs=Pb[pbi][:, kk, :], start=(kk == 0), stop=(kk == 1))
                            return ins
                        accv = acc[:, :, q0:q0 + 127 * d + 1:d]
                        if g == 0:
                            fin = (lambda e, accv=accv, Ov=Ov: e.tensor_copy(out=accv, in_=Ov))
                        else:
                            fin = (lambda e, accv=accv, Ov=Ov: e.tensor_tensor(out=accv, in0=accv, in1=Ov, op=ALU.add))
                        cur = (pv, [R_Pb[pbi], R_vsb[hi], R_vT], [bankR[ob]], fin, [bankR[ob], R_acc], [R_acc])
                        if pending is not None:
                            P.op("pe", pending[0], pending[1], pending[2])
                            P.op("dve", pending[3], pending[4], pending[5])
                        pending = cur
            if pending is not None:
                P.op("pe", pending[0], pending[1], pending[2])
                P.op("dve", pending[3], pending[4], pending[5])
                pending = None
            P.op("dve", lambda e: e.reciprocal(out=acc[:, 1, :], in_=acc[:, 1, :]), [R_acc], [R_acc])
            P.op("dve", lambda e, j=j: e.tensor_tensor(out=attnT[:, j, :], in0=acc[:, 0, :], in1=acc[:, 1, :], op=ALU.mult),
                 [R_acc], [R_attnT[j]])
        if debug:
            s_dbg = P.new_dsem("dbg")
            P.dma("sp", dbg["attnT"], attnT.rearrange("p a b -> p (a b)"), R_attnT, [R_dbg], s_dbg)
        P.barrier()
        if stop_after == 2:
            P.emit()
            return nc
        AR.off = PERSIST

        sguT = AR.alloc([32, 512], BF16)
        R_sguT = [Res("sguT%d" % i) for i in range(4)]
        mergedT = AR.alloc([32, 512], BF16)
        R_mT = [Res("mT%d" % i) for i in range(8)]
        WsT = AR.alloc([16, 128], BF16)
        R_WsT = Res("WsT")
        s_WsT = P.new_dsem("WsT")
        sgub = AR.alloc([16, 128], BF16)
        R_sgub = Res("sgub")
        s_sgub = P.new_dsem("sgub")
        rstd_vs = AR.alloc([16], F32)
        R_rvs = Res("rstd_vs")
        VAR35 = AR.off
        sgn_bc = AR.alloc([D], F32)
        R_sgn = Res("sgn")
        s_sgn = P.new_dsem("sgn")
        gvs = [AR.alloc([D], BF16) for _ in range(2)]
        R_gvsb = [Res("gvsb%d" % i) for i in range(2)]
        s_gvsb = [P.new_dsem("gvsb%d" % i) for i in range(2)]
        vsn = [AR.alloc([D], BF16) for _ in range(2)]
        R_vsn = [Res("vsn%d" % i) for i in range(2)]
        guT = [AR.alloc([32, 128], BF16) for _ in range(2)]
        R_guT = [Res("guT%d" % i) for i in range(2)]
        s_guT = [P.new_dsem("guT%d" % i) for i in range(2)]
        print("phase3 arena", AR.off)
        AR.off = VAR35
        wsl_flat = [AR.alloc([32 * 512], BF16) for _ in range(2)]
        wsl = [w[:, 0:40 * 256].rearrange("p (a b) -> p a b", a=40) for w in wsl_flat]
        wsl5 = [w.rearrange("p (a b) -> p a b", a=32) for w in wsl_flat]
        R_wsl = [Res("wsl%d" % i) for i in range(2)]
        s_wsl = [P.new_dsem("wsl%d" % i) for i in range(2)]
        sga = [AR.alloc([512], BF16) for _ in range(2)]
        sgb = [AR.alloc([512], BF16) for _ in range(2)]
        R_sgab = [Res("sga%d" % i) for i in range(2)]
        R_sgbb = [Res("sgb%d" % i) for i in range(2)]
        s_sgab = [P.new_dsem("sga%d" % i) for i in range(2)]
        s_sgbb = [P.new_dsem("sgb%d" % i) for i in range(2)]
        t1 = [AR.alloc([512], F32) for _ in range(2)]
        t2 = [AR.alloc([512], F32) for _ in range(2)]
        R_t1 = [Res("t1%d" % i) for i in range(2)]
        R_t2 = [Res("t2%d" % i) for i in range(2)]
        xr = [AR.alloc([512], F32) for _ in range(2)]
        R_xr = [Res("xr%d" % i) for i in range(2)]
        s_xr = [P.new_dsem("xr%d" % i) for i in range(2)]
        xo = [AR.alloc([512], F32) for _ in range(2)]
        R_xo = [Res("xo%d" % i) for i in range(2)]
        s_xo = [P.new_dsem("xo%d" % i) for i in range(2)]
        print("phase4-5 arena", AR.off)

        P.dma("pool", WsT.rearrange("p a b -> p (a b)"), sgu_wT, [R_in], [R_WsT], s_WsT)
        P.dma("pool", sgub.rearrange("p a b -> p (a b)")[0:1, :], sgu_b, [R_in], [R_sgub], s_sgub)
        P.op("dve", lambda e: e.tensor_reduce(out=rstd_vs, in_=ssq_vs, axis=mybir.AxisListType.X, op=ALU.add), [R_ssqvs], [R_rvs])
        P.op("dve", lambda e: e.tensor_scalar(out=rstd_vs, in0=rstd_vs, scalar1=1.0 / D, scalar2=EPS, op0=ALU.mult, op1=ALU.add), [R_rvs], [R_rvs])
        P.op("act", lambda e: e.activation(out=rstd_vs, in_=rstd_vs, func=AF.Sqrt), [R_rvs], [R_rvs])
        P.op("dve", lambda e: e.reciprocal(out=rstd_vs, in_=rstd_vs), [R_rvs], [R_rvs])

        c35 = {"ws": 0, "bank": 0, "ch": 0, "sg": 0, "t": 0, "x": 0}
        for b in range(4):
            tb0 = b * 512
            load_bc(n_sgu, sgn_bc, R_sgn, s_sgn)
            for c in range(4):
                ci = c35["ch"] % 2
                c35["ch"] += 1
                cidx = b * 4 + c
                t0 = tb0 + c * 128
                P.dma("sp", gvs[ci], gvs_s[t0:t0 + 128, :], [R_gvs], [R_gvsb[ci]], s_gvsb[ci])
                P.dma("sp", guT[ci], guT_s[:, t0:t0 + 128].rearrange("(cb p) t -> p cb t", p=128), [R_gu], [R_guT[ci]], s_guT[ci])
                P.op("dve", lambda e, ci=ci, cidx=cidx: e.scalar_tensor_tensor(out=vsn[ci], in0=gvs[ci], scalar=rstd_vs[:, cidx:cidx + 1], in1=sgn_bc,
                                                                                op0=ALU.mult, op1=ALU.mult),
                     [R_gvsb[ci], R_rvs, R_sgn], [R_vsn[ci]])
                for cb4 in range(8):
                    bi = c35["bank"] % 6
                    c35["bank"] += 1
                    bv = bank_f32(bi).rearrange("p (a b) -> p a b", a=4)

                    def sg(e, bv=bv, cb4=cb4, ci=ci):
                        ins = None
                        for cbi in range(4):
                            cb = cb4 * 4 + cbi
                            g = cb // 2
                            e.matmul(bv[:, cbi, :], lhsT=ones_bf[0:1, :], rhs=sgub[0:1, g, :], start=True, stop=False)
                            ins = e.matmul(bv[:, cbi, :], lhsT=vsn[ci][:, cb * 128:(cb + 1) * 128], rhs=WsT[:, g, :], start=False, stop=True)
                        return ins
                    P.op("pe", sg, [R_cst, R_sgub, R_vsn[ci], R_WsT], [bankR[bi]])
                    dst = sguT[:, cb4 * 4:(cb4 + 1) * 4, c * 128:(c + 1) * 128]
                    P.op("dve", lambda e, dst=dst, bv=bv, ci=ci, cb4=cb4: e.tensor_tensor(out=dst, in0=bv, in1=guT[ci][:, cb4 * 4:(cb4 + 1) * 4, :], op=ALU.mult),
                         [bankR[bi], R_guT[ci]], [R_sguT[c]])
            P.barrier()
            for cg in range(16):
                si = c35["ws"] % 2
                c35["ws"] += 1
                P.dma("act", wsl_flat[si][:, 0:40 * 256], w_br_bf[cg], [R_wbf], [R_wsl[si]], s_wsl[si])
                ws = wsl[si]
                for fb in range(2):
                    ba = c35["bank"] % 6
                    bb = (c35["bank"] + 1) % 6
                    c35["bank"] += 2
                    f0 = cg * 256 + fb * 128
                    gi = c35["sg"] % 2
                    c35["sg"] += 1
                    P.dma("sp", sga[gi], sgaT_s[f0:f0 + 128, tb0:tb0 + 512], [R_sga], [R_sgab[gi]], s_sgab[gi])
                    P.dma("sp", sgb[gi], sgbT_s[f0:f0 + 128, tb0:tb0 + 512], [R_sgb], [R_sgbb[gi]], s_sgbb[gi])

                    def mma(e, ba=ba, ws=ws, fb=fb, tb0=tb0):
                        ins = None
                        for kc in range(8):
                            ins = e.matmul(bank_f32(ba), lhsT=ws[:, kc, fb * 128:(fb + 1) * 128], rhs=attnT[:, kc, tb0:tb0 + 512], start=(kc == 0), stop=(kc == 7))
                        return ins

                    def mmb(e, bb=bb, ws=ws, fb=fb):
                        ins = None
                        for kc in range(32):
                            ins = e.matmul(bank_f32(bb), lhsT=ws[:, 8 + kc, fb * 128:(fb + 1) * 128], rhs=sguT[:, kc, :], start=(kc == 0), stop=(kc == 31))
                        return ins
                    P.op("pe", mma, R_attnT + [R_wsl[si]], [bankR[ba]])
                    P.op("pe", mmb, R_sguT + [R_wsl[si]], [bankR[bb]])
                    P.op("dve", lambda e, gi=gi, ba=ba: e.tensor_tensor(out=t1[gi], in0=bank_f32(ba), in1=sga[gi], op=ALU.mult),
                         [bankR[ba], R_sgab[gi]], [R_t1[gi]])
                    P.op("dve", lambda e, gi=gi, bb=bb: e.tensor_tensor(out=t2[gi], in0=bank_f32(bb), in1=sgb[gi], op=ALU.mult),
                         [bankR[bb], R_sgbb[gi]], [R_t2[gi]])
                    mdst = mergedT[:, cg * 2 + fb, :]
                    P.op("pool", lambda e, gi=gi, mdst=mdst: e.tensor_tensor(out=mdst, in0=t1[gi], in1=t2[gi], op=ALU.add),
                         [R_t1[gi], R_t2[gi]], [R_mT[cg // 2]])
            for cg in range(8):
                si = c35["ws"] % 2
                c35["ws"] += 1
                P.dma("act", wsl_flat[si], w_out_bf[cg], [R_wbf], [R_wsl[si]], s_wsl[si])
                ws = wsl5[si]
                for tt in range(4):
                    bi = c35["bank"] % 6
                    c35["bank"] += 1
                    xi = c35["x"] % 2
                    c35["x"] += 1
                    t0 = tb0 + tt * 128
                    P.dma("sp", xr[xi], x_main[t0:t0 + 128, cg * 512:(cg + 1) * 512], [R_in], [R_xr[xi]], s_xr[xi])

                    def mmo(e, bi=bi, ws=ws, tt=tt):
                        ins = None
                        for kc in range(32):
                            ins = e.matmul(bank_f32(bi), lhsT=mergedT[:, kc, tt * 128:(tt + 1) * 128], rhs=ws[:, kc, :], start=(kc == 0), stop=(kc == 31))
                        return ins
                    P.op("pe", mmo, R_mT + [R_wsl[si]], [bankR[bi]])
                    P.op("dve", lambda e, xi=xi, bi=bi: e.tensor_tensor(out=xo[xi], in0=bank_f32(bi), in1=xr[xi], op=ALU.add),
                         [bankR[bi], R_xr[xi]], [R_xo[xi]])
                    P.dma("sp", x1_s[t0:t0 + 128, cg * 512:(cg + 1) * 512], xo[xi], [R_xo[xi]], [R_x1], s_xo[xi])
            P.barrier()
            if stop_after == 3 + 0.1 * b:
                P.emit()
                return nc
        if stop_after == 5:
            P.emit()
            return nc
        AR.off = PERSIST0

        h2T = AR.alloc([32, 1024], BF16)
        R_h2T = [Res("h2T%d" % i) for i in range(8)]
        ws6 = [AR.alloc([64, 256], BF16) for _ in range(2)]
        R_ws6 = [Res("ws6%d" % i, multi=True) for i in range(2)]
        s_ws6 = [P.new_dsem("ws6%d" % i) for i in range(2)]
        xs6 = [AR.alloc([D], F32) for _ in range(2)]
        R_xs6 = [Res("xs6%d" % i) for i in range(2)]
        s_xs6 = [P.new_dsem("xs6%d" % i) for i in range(2)]
        xn6 = [AR.alloc([D], BF16) for _ in range(2)]
        R_xn6 = [Res("xn6%d" % i) for i in range(2)]
        tg6 = [AR.alloc([512], F32) for _ in range(2)]
        R_tg6 = [Res("tg6%d" % i) for i in range(2)]
        sta = [AR.alloc([512], BF16) for _ in range(3)]
        R_sta = [Res("sta%d" % i) for i in range(3)]
        s_sta = [P.new_dsem("sta%d" % i) for i in range(3)]
        print("phase6 arena", AR.off)
        P.dma("sp", gcol, n_ffn, [R_in], [R_gcol], s_gcol)
        c6 = {"ws": 0, "bank": 0, "x": 0, "t": 0, "st": 0}
        for B in range(2):
            for tt in range(8):
                i = c6["x"] % 2
                c6["x"] += 1
                t0 = B * 1024 + tt * 128
                prologue_tile(x1_s[t0:t0 + 128, :], R_x1, xs6[i], R_xs6[i], s_xs6[i], xn6[i], R_xn6[i],
                              h2T, R_h2T[tt], tt * 128, 2 * i, (6, 7))
            for cgf in range(43):
                si = c6["ws"] % 2
                c6["ws"] += 1
                P.dma("pool", ws6[si][:, 0:32, :].rearrange("p a b -> p (a b)"), w_gate[cgf], [R_in], [R_ws6[si]], s_ws6[si])
                P.dma("pool", ws6[si][:, 32:64, :].rearrange("p a b -> p (a b)"), w_up[cgf], [R_in], [R_ws6[si]], s_ws6[si])
                ws = ws6[si]
                for fb in range(2):
                    for th in range(2):
                        bg = c6["bank"] % 6
                        bu = (c6["bank"] + 1) % 6
                        c6["bank"] += 2

                        def mmg(e, bg=bg, ws=ws, fb=fb, th=th):
                            ins = None
                            for kc in range(32):
                                ins = e.matmul(bank_f32(bg), lhsT=ws[:, kc, fb * 128:(fb + 1) * 128], rhs=h2T[:, kc, th * 512:(th + 1) * 512], start=(kc == 0), stop=(kc == 31))
                            return ins

                        def mmu(e, bu=bu, ws=ws, fb=fb, th=th):
                            ins = None
                            for kc in range(32):
                                ins = e.matmul(bank_f32(bu), lhsT=ws[:, 32 + kc, fb * 128:(fb + 1) * 128], rhs=h2T[:, kc, th * 512:(th + 1) * 512], start=(kc == 0), stop=(kc == 31))
                            return ins
                        rd = [R_h2T[th * 4 + i] for i in range(4)] + [R_ws6[si]]
                        P.op("pe", mmg, rd, [bankR[bg]])
                        P.op("pe", mmu, rd, [bankR[bu]])
                        ti = c6["t"] % 2
                        c6["t"] += 1
                        s = c6["st"] % 3
                        c6["st"] += 1
                        P.op("act", lambda e, ti=ti, bg=bg: e.activation(out=tg6[ti], in_=bank_f32(bg), func=AF.Silu), [bankR[bg]], [R_tg6[ti]])
                        P.op("dve", lambda e, ti=ti, bu=bu, s=s: e.tensor_tensor(out=sta[s], in0=bank_f32(bu), in1=tg6[ti], op=ALU.mult),
                             [bankR[bu], R_tg6[ti]], [R_sta[s]])
                        f0 = cgf * 256 + fb * 128
                        c0 = B * 1024 + th * 512
                        P.dma("sp", actT_s[f0:f0 + 128, c0:c0 + 512], sta[s], [R_sta[s]], [R_act], s_sta[s])
        P.barrier()
        if stop_after == 6:
            P.emit()
            return nc
        AR.off = PERSIST0

        A7 = AR.alloc([43, 1024], BF16)
        R_A7 = [Res("A7%d" % i) for i in range(4)]
        s_A7 = [P.new_dsem("A7%d" % i) for i in range(4)]
        ws7 = [AR.alloc([43, 512], BF16) for _ in range(2)]
        R_ws7 = [Res("ws7%d" % i) for i in range(2)]
        s_ws7 = [P.new_dsem("ws7%d" % i) for i in range(2)]
        xr7 = [AR.alloc([512], F32) for _ in range(2)]
        R_xr7 = [Res("xr7%d" % i) for i in range(2)]
        s_xr7 = [P.new_dsem("xr7%d" % i) for i in range(2)]
        xo7 = [AR.alloc([512], F32) for _ in range(2)]
        R_xo7 = [Res("xo7%d" % i) for i in range(2)]
        s_xo7 = [P.new_dsem("xo7%d" % i) for i in range(2)]
        print("phase7 arena", AR.off)
        c7 = {"ws": 0, "bank": 0, "x": 0}
        for hh in range(2):
            src_s, R_src = (x1_s, R_x1) if hh == 0 else (p1_s, R_p1)
            dst_s, R_dsts = (p1_s, R_p1) if hh == 0 else (x2_s, R_x2)
            for B in range(2):
                for g4 in range(4):
                    P.dma("act", A7[:, :, g4 * 256:(g4 + 1) * 256],
                          actT_s[hh * 5504:(hh + 1) * 5504, B * 1024 + g4 * 256:B * 1024 + (g4 + 1) * 256].rearrange("(kc p) t -> p kc t", p=128),
                          [R_act], [R_A7[g4]], s_A7[g4])
                for cg in range(8):
                    si = c7["ws"] % 2
                    c7["ws"] += 1
                    P.dma("pool", ws7[si].rearrange("p a b -> p (a b)"), w_down[hh * 8 + cg], [R_in], [R_ws7[si]], s_ws7[si])
                    ws = ws7[si]
                    for tt in range(8):
                        bi = c7["bank"] % 6
                        c7["bank"] += 1
                        xi = c7["x"] % 2
                        c7["x"] += 1
                        t0 = B * 1024 + tt * 128
                        P.dma("sp", xr7[xi], src_s[t0:t0 + 128, cg * 512:(cg + 1) * 512], [R_src], [R_xr7[xi]], s_xr7[xi])

                        def mmd(e, bi=bi, ws=ws, tt=tt):
                            ins = None
                            for kc in range(43):
                                ins = e.matmul(bank_f32(bi), lhsT=A7[:, kc, tt * 128:(tt + 1) * 128], rhs=ws[:, kc, :], start=(kc == 0), stop=(kc == 42))
                            return ins
                        P.op("pe", mmd, [R_A7[tt // 2], R_ws7[si]], [bankR[bi]])
                        P.op("dve", lambda e, xi=xi, bi=bi: e.tensor_tensor(out=xo7[xi], in0=bank_f32(bi), in1=xr7[xi], op=ALU.add),
                             [bankR[bi], R_xr7[xi]], [R_xo7[xi]])
                        P.dma("sp", dst_s[t0:t0 + 128, cg * 512:(cg + 1) * 512], xo7[xi], [R_xo7[xi]], [R_dsts], s_xo7[xi])
        P.barrier()
        AR.off = PERSIST0

        fbc = AR.alloc([D], F32)
        R_fbc = Res("fbc")
        s_fbc = P.new_dsem("fbc")
        xs8 = [AR.alloc([D], F32) for _ in range(2)]
        R_xs8 = [Res("xs8%d" % i) for i in range(2)]
        s_xs8 = [P.new_dsem("xs8%d" % i) for i in range(2)]
        ys8 = [AR.alloc([D], F32) for _ in range(2)]
        R_ys8 = [Res("ys8%d" % i) for i in range(2)]
        s_ys8 = [P.new_dsem("ys8%d" % i) for i in range(2)]
        load_bc(n_fin, fbc, R_fbc, s_fbc)
        for tt in range(16):
            i = tt % 2
            P.dma("sp", xs8[i], x2_s[tt * 128:(tt + 1) * 128, :], [R_x2], [R_xs8[i]], s_xs8[i])
            ssq = small[:, 2 * i:2 * i + 1]
            rstd = small[:, 2 * i + 1:2 * i + 2]
            R_st = Res("st8")
            P.op("act", lambda e, i=i, ssq=ssq: e.activation(out=ys8[i], in_=xs8[i], func=AF.Square, accum_out=ssq), [R_xs8[i]], [R_ys8[i], R_st])
            P.op("dve", lambda e, ssq=ssq, rstd=rstd: e.tensor_scalar(out=rstd, in0=ssq, scalar1=1.0 / D, scalar2=EPS, op0=ALU.mult, op1=ALU.add), [R_st], [R_st])
            P.op("act", lambda e, rstd=rstd: e.activation(out=rstd, in_=rstd, func=AF.Sqrt), [R_st], [R_st])
            P.op("dve", lambda e, rstd=rstd: e.reciprocal(out=rstd, in_=rstd), [R_st], [R_st])
            P.op("dve", lambda e, i=i, rstd=rstd: e.scalar_tensor_tensor(out=ys8[i], in0=xs8[i], scalar=rstd, in1=fbc, op0=ALU.mult, op1=ALU.mult),
                 [R_xs8[i], R_st, R_fbc], [R_ys8[i]])
            P.dma("sp", y_out[tt * 128:(tt + 1) * 128, :], ys8[i], [R_ys8[i]], [R_y], s_ys8[i])
        P.barrier()
        P.emit()
    return nc


def _rope_table(pos):
    inv_freq = (np.float32(500000.0) ** (-np.arange(0, 32, 2, dtype=np.float32) / np.float32(32))).astype(np.float32)
    ang = (pos[:, None].astype(np.float32) * inv_freq[None, :]).astype(np.float32)
    c = np.cos(ang).astype(np.float32)
    s = np.sin(ang).astype(np.float32)
    tab = np.concatenate([c, c, -s, s], axis=1)
    return np.ascontiguousarray(tab.reshape(24, 128, 64).transpose(1, 0, 2).reshape(128, 24 * 64))


def _valid_cols(halo_valid):
    out = np.zeros((128, 69), np.float32)
    idx = 0
    i = np.arange(128)
    for g, d in enumerate(DILS):
        nkt = NTOK // d // 128 + 1
        for r in range(d):
            for kt in range(nkt):
                t = r + d * (-64 + 128 * kt + i)
                ok = (t >= 0) & ((t < NTOK) | halo_valid)
                out[:, idx] = ok.astype(np.float32)
                idx += 1
    assert idx == 69
    return out


def _consts():
    i = np.arange(128)
    ident = np.eye(128, dtype=np.float32)
    m0 = np.where(i[:, None] >= i[None, :], 0.0, MASKV).astype(np.float32)
    m1 = np.where(i[:, None] <= i[None, :], 0.0, MASKV).astype(np.float32)
    ones = np.ones((128, 128), np.float32)
    return np.ascontiguousarray(np.concatenate([ident, m0, m1, ones], axis=1))


_NC_CACHE = {}


def make_in_maps(inputs):
    f = lambda a: np.ascontiguousarray(np.asarray(a, dtype=np.float32))
    xp = f(inputs["x_prompt"])
    xsm = f(inputs["x_sample"])
    def tile_w(w, kc, nw):
        K, N = w.shape
        assert K == kc * 128 and N % nw == 0
        return np.ascontiguousarray(w.reshape(kc, 128, N // nw, nw).transpose(2, 1, 0, 3)).reshape(N // nw, 128, kc * nw)
    wd = f(inputs["w_down"])[0]
    shared = {
        "w_in": tile_w(f(inputs["w_in"])[0], 32, 512), "w_branch": tile_w(f(inputs["w_branch"])[0], 40, 256),
        "w_out": tile_w(f(inputs["w_out"])[0], 32, 512),
        "w_gate": tile_w(f(inputs["w_gate"])[0], 32, 256), "w_up": tile_w(f(inputs["w_up"])[0], 32, 256),
        "w_down": np.concatenate([tile_w(wd[0:5504], 43, 512), tile_w(wd[5504:11008], 43, 512)], axis=0),
        "attn_norm": np.ascontiguousarray(f(inputs["attn_norm"])[0].reshape(32, 128).T),
        "ffn_norm": np.ascontiguousarray(f(inputs["ffn_norm"])[0].reshape(32, 128).T),
        "sgu_norm": f(inputs["sgu_norm"])[0], "final_norm": f(inputs["final_norm"]),
        "consts": _consts(),
    }
    sw = f(inputs["sgu_w"])[0]
    sb = f(inputs["sgu_b"])[0]
    wT_fwd = np.ascontiguousarray(sw.transpose(2, 0, 1).reshape(128, 16 * 128))
    wT_rev = np.ascontiguousarray(sw[:, ::-1, ::-1].transpose(2, 0, 1).reshape(128, 16 * 128))
    b_fwd = np.ascontiguousarray(sb.reshape(1, 16 * 128))
    b_rev = np.ascontiguousarray(sb[:, ::-1].reshape(1, 16 * 128))
    zeros_halo = np.zeros((HALO, D), np.float32)
    maps = []
    for c in range(8):
        m = dict(shared)
        if c < 4:
            m["x_main"] = xp[c]
            m["x_halo"] = zeros_halo
            pos = np.arange(3072, dtype=np.float32)
            hv = False
            rev = False
        else:
            sq = (c - 4) // 2
            rev = (c - 4) % 2 == 1
            xx = xsm[sq][::-1] if rev else xsm[sq]
            m["x_main"] = np.ascontiguousarray(xx[0:NTOK])
            m["x_halo"] = np.ascontiguousarray(xx[NTOK:NTOK + HALO])
            pos = np.arange(3072, dtype=np.float32)
            if rev:
                pos = (4095.0 - pos).astype(np.float32)
            hv = True
        m["rope"] = _rope_table(pos)
        m["validcol"] = _valid_cols(hv)
        m["sgu_wT"] = wT_rev if rev else wT_fwd
        m["sgu_b"] = b_rev if rev else b_fwd
        maps.append(m)
    return maps


def assemble(results):
    yp = np.stack([np.asarray(results[c]["y"], dtype=np.float32) for c in range(4)], axis=0)
    ys = []
    for sq in range(2):
        a = np.asarray(results[4 + 2 * sq]["y"], dtype=np.float32)
        b = np.asarray(results[5 + 2 * sq]["y"], dtype=np.float32)[::-1]
        ys.append(np.concatenate([a, b], axis=0))
    return yp, np.stack(ys, axis=0)


def kernel(**inputs):
    if "nc" not in _NC_CACHE:
        _NC_CACHE["nc"] = build_program(False)
    nc = _NC_CACHE["nc"]
    maps = make_in_maps(inputs)
    res = run_bass_kernel_spmd(nc, maps, core_ids=list(range(8)))
    yp, ys = assemble(res.results)
    return (np.ascontiguousarray(yp), np.ascontiguousarray(ys))
```
